# Optimizing a Trainium2 kernel written in Bass

```python
import math
import jax, jax.numpy as jnp
from jax import lax
import numpy as np

D_MODEL = 1024
BATCH = 8
SEQ = 4096
DEPTH = 2

GRID_W = 64
CTX_LEN = 256
MIX_WIDTH = 2 * D_MODEL
SSD_WIDTH = D_MODEL
SSD_HEAD_DIM = 64
SSD_HEADS = SSD_WIDTH // SSD_HEAD_DIM
SSD_GROUPS = 2
SSD_HPG = SSD_HEADS // SSD_GROUPS
SSD_STATE = 128
SSD_CHUNK = 128
CONV_K = 3
CONV_CH = SSD_WIDTH + 2 * SSD_GROUPS * SSD_STATE
GLA_HEADS = 4
GLA_V_WIDTH = MIX_WIDTH - SSD_WIDTH
GLA_K_WIDTH = GLA_V_WIDTH // 2
GLA_DK = GLA_K_WIDTH // GLA_HEADS
GLA_DV = GLA_V_WIDTH // GLA_HEADS
GLA_GATE_RANK = 16
GLA_GATE_NORM = 16.0
GLA_CHUNK = 64
D_FF = 4 * D_MODEL
EPS = 1e-6
IN_COLS = (2 * SSD_WIDTH + 2 * SSD_GROUPS * SSD_STATE + 2 * SSD_HEADS
           + 2 * GLA_K_WIDTH + 2 * GLA_V_WIDTH + 2 * GLA_GATE_RANK)

kernel_name = 'hybrid_ssd_gla_prefix_dit_block'


def _in_split_points():
    widths = [SSD_WIDTH, SSD_WIDTH, SSD_GROUPS * SSD_STATE, SSD_GROUPS * SSD_STATE,
              SSD_HEADS, SSD_HEADS, GLA_K_WIDTH, GLA_K_WIDTH, GLA_V_WIDTH, GLA_V_WIDTH,
              GLA_GATE_RANK, GLA_GATE_RANK]
    pts, acc = [], 0
    for w in widths[:-1]:
        acc += w
        pts.append(acc)
    return pts


def rms_norm(x, g):
    xf = x.astype(jnp.float32)
    y = xf * lax.rsqrt(jnp.mean(xf * xf, axis=-1, keepdims=True) + EPS)
    return (y * g.astype(jnp.float32)).astype(x.dtype)


def modulate(h, shift, scale):
    return h * (1 + scale) + shift


def dw_conv_grid(u, w, bias, rows, cols):
    b, L, ch = u.shape
    img = u.reshape(b, rows, cols, ch)
    out = lax.conv_general_dilated(img, w[:, :, None, :].astype(u.dtype), window_strides=(1, 1),
                                   padding='SAME', dimension_numbers=('NHWC', 'HWIO', 'NHWC'),
                                   feature_group_count=ch)
    return (out + bias.astype(u.dtype)).reshape(b, L, ch)


def ssd_chunked_scan(xh, dt, A, Bm, Cm, h0):
    f32 = jnp.float32
    b, L, G, HG, P = xh.shape
    N = Bm.shape[-1]
    Q = SSD_CHUNK
    nc = L // Q
    x = xh.astype(f32).reshape(b, nc, Q, G, HG, P)
    dt = dt.reshape(b, nc, Q, G, HG)
    Bc = Bm.astype(f32).reshape(b, nc, Q, G, N)
    Cc = Cm.astype(f32).reshape(b, nc, Q, G, N)
    a_cum = jnp.cumsum(dt * A, axis=2)
    xdt = x * dt[..., None]
    a_t = jnp.moveaxis(a_cum, 2, -1)
    diff = a_t[..., :, None] - a_t[..., None, :]
    lower = jnp.tril(jnp.ones((Q, Q), dtype=bool))
    decay = jnp.exp(jnp.where(lower, diff, -jnp.inf))
    cb = jnp.einsum('bcign,bcjgn->bcgij', Cc, Bc)
    y_diag = jnp.einsum('bcghij,bcjghp->bcighp', cb[:, :, :, None] * decay, xdt)
    decay_to_end = jnp.exp(a_cum[:, :, -1:] - a_cum)
    states = jnp.einsum('bcjgn,bcjghp->bcghpn', Bc, xdt * decay_to_end[..., None])
    chunk_decay = jnp.exp(a_cum[:, :, -1])

    def step(h, inp):
        dec, st = inp
        return dec[..., None, None] * h + st, h

    h_final, h_prev = lax.scan(step, h0, (jnp.moveaxis(chunk_decay, 1, 0), jnp.moveaxis(states, 1, 0)))
    h_prev = jnp.moveaxis(h_prev, 0, 1)
    y_off = jnp.einsum('bcign,bcghpn->bcighp', Cc, h_prev) * jnp.exp(a_cum)[..., None]
    return (y_diag + y_off).reshape(b, L, G, HG, P), h_final


def gla_chunked_scan(q, k, v, log_a, S0):
    b, L, H, DK = q.shape
    DV = v.shape[-1]
    C = GLA_CHUNK
    nc = L // C
    q = q.reshape(b, nc, C, H, DK)
    k = k.reshape(b, nc, C, H, DK)
    v = v.reshape(b, nc, C, H, DV)
    bcum = jnp.cumsum(log_a.reshape(b, nc, C, H, DK), axis=2)
    b_last = bcum[:, :, -1:]
    q_dec = q * jnp.exp(bcum)
    k_inv = k * jnp.exp(-bcum)
    k_end = k * jnp.exp(b_last - bcum)
    lower = jnp.tril(jnp.ones((C, C), dtype=bool))
    att = jnp.where(lower, jnp.einsum('bcihd,bcjhd->bchij', q_dec, k_inv), 0.0)
    o_intra = jnp.einsum('bchij,bcjhv->bcihv', att, v)
    U = jnp.einsum('bcjhd,bcjhv->bchdv', k_end, v)
    dec = jnp.exp(b_last[:, :, 0])

    def step(S, inp):
        a, u_ = inp
        return a[..., None] * S + u_, S

    S_T, S_prev = lax.scan(step, S0, (jnp.moveaxis(dec, 1, 0), jnp.moveaxis(U, 1, 0)))
    S_prev = jnp.moveaxis(S_prev, 0, 1)
    o_inter = jnp.einsum('bcihd,bchdv->bcihv', q_dec, S_prev)
    return (o_intra + o_inter).reshape(b, L, H, DV), S_T


def hybrid_mixer(u, rows, cols, init, w_in, conv_w, conv_b, dt_bias, a_log, d_skip,
                 ssd_norm_g, gla_w2, gla_b2, gla_norm_g):
    f32 = jnp.float32
    b, L, _ = u.shape
    flip = lambda t: jnp.flip(t, axis=1)
    proj = u @ w_in
    z, xs, Bm, Cm, dt_f, dt_b, q, k, v, r, ga_f, ga_b = jnp.split(proj, _in_split_points(), axis=-1)

    xbc = jax.nn.silu(dw_conv_grid(jnp.concatenate([xs, Bm, Cm], axis=-1), conv_w, conv_b, rows, cols))
    xs, Bm, Cm = jnp.split(xbc, [SSD_WIDTH, SSD_WIDTH + SSD_GROUPS * SSD_STATE], axis=-1)
    xh = xs.reshape(b, L, SSD_GROUPS, SSD_HPG, SSD_HEAD_DIM)
    Bm = Bm.reshape(b, L, SSD_GROUPS, SSD_STATE)
    Cm = Cm.reshape(b, L, SSD_GROUPS, SSD_STATE)

    def ssd_direction(d, dt_raw, xh_, B_, C_, h0):
        dt = jax.nn.softplus(dt_raw.astype(f32) + dt_bias[d].astype(f32)).reshape(b, L, SSD_GROUPS, SSD_HPG)
        A = -jnp.exp(a_log[d].astype(f32)).reshape(SSD_GROUPS, SSD_HPG)
        y, hT = ssd_chunked_scan(xh_, dt, A, B_, C_, h0)
        return y + d_skip[d].astype(f32).reshape(SSD_GROUPS, SSD_HPG, 1) * xh_.astype(f32), hT

    y_f, hs_f = ssd_direction(0, dt_f, xh, Bm, Cm, init[0])
    y_b, hs_b = ssd_direction(1, flip(dt_b), flip(xh), flip(Bm), flip(Cm), init[1])
    y = (y_f + flip(y_b)).reshape(b, L, SSD_WIDTH) * jax.nn.silu(z.astype(f32))
    y = rms_norm(y.reshape(b, L, SSD_GROUPS, SSD_WIDTH // SSD_GROUPS),
                 ssd_norm_g.reshape(SSD_GROUPS, SSD_WIDTH // SSD_GROUPS)).reshape(b, L, SSD_WIDTH)

    qh = q.astype(f32).reshape(b, L, GLA_HEADS, GLA_DK) * (GLA_DK ** -0.5)
    kh = k.astype(f32).reshape(b, L, GLA_HEADS, GLA_DK)
    vh = v.astype(f32).reshape(b, L, GLA_HEADS, GLA_DV)

    def gla_direction(d, ga, q_, k_, v_, S0):
        log_a = jax.nn.log_sigmoid((ga @ gla_w2[d] + gla_b2[d]).astype(f32)) / GLA_GATE_NORM
        return gla_chunked_scan(q_, k_, v_, log_a.reshape(b, L, GLA_HEADS, GLA_DK), S0)

    o_f, S_f = gla_direction(0, ga_f, qh, kh, vh, init[2])
    o_b, S_b = gla_direction(1, flip(ga_b), flip(qh), flip(kh), flip(vh), init[3])
    o = rms_norm(o_f + flip(o_b), gla_norm_g).reshape(b, L, GLA_V_WIDTH) * jax.nn.silu(r.astype(f32))

    heads = jnp.concatenate([y, o], axis=-1).astype(u.dtype)
    return heads, (hs_f, hs_b, S_f, S_b)


def sq_relu_mlp(h, w1, w2):
    return jnp.square(jax.nn.relu(h @ w1)) @ w2


def setup_inputs(seed: int = 0) -> dict:
    key = jax.random.key(seed)
    ks = jax.random.split(key, 24)
    D = D_MODEL
    f32 = jnp.float32

    def nrm(k, shape, scale):
        return jax.random.normal(k, shape, f32) * scale

    dt0 = jnp.exp(jax.random.uniform(ks[10], (DEPTH, 2, SSD_HEADS), f32)
                  * (math.log(0.1) - math.log(0.001)) + math.log(0.001))
    return {
        'x': nrm(ks[0], (BATCH, SEQ, D), 1.0),
        'c': nrm(ks[1], (BATCH, D), 1.0),
        'ctx': nrm(ks[2], (BATCH, CTX_LEN, D), 1.0),
        'c_ctx': nrm(ks[3], (D,), 1.0),
        'w_ada': nrm(ks[4], (DEPTH, D, 6 * D), 0.5 * D ** -0.5),
        'b_ada': nrm(ks[5], (DEPTH, 6 * D), 0.02),
        'norm1_g': 1.0 + nrm(ks[6], (DEPTH, D), 0.02),
        'w_in': nrm(ks[7], (DEPTH, D, IN_COLS), D ** -0.5),
        'conv_w': nrm(ks[8], (DEPTH, CONV_K, CONV_K, CONV_CH), 1.0 / CONV_K),
        'conv_b': nrm(ks[9], (DEPTH, CONV_CH), 0.02),
        'dt_bias': dt0 + jnp.log(-jnp.expm1(-dt0)),
        'a_log': jnp.log(jax.random.uniform(ks[11], (DEPTH, 2, SSD_HEADS), f32, minval=1.0, maxval=16.0)),
        'd_skip': 1.0 + nrm(ks[12], (DEPTH, 2, SSD_HEADS), 0.02),
        'ssd_norm_g': 1.0 + nrm(ks[13], (DEPTH, SSD_WIDTH), 0.02),
        'gla_w2': nrm(ks[14], (DEPTH, 2, GLA_GATE_RANK, GLA_K_WIDTH), GLA_GATE_RANK ** -0.5),
        'gla_b2': nrm(ks[15], (DEPTH, 2, GLA_K_WIDTH), 0.02),
        'gla_norm_g': 1.0 + nrm(ks[16], (DEPTH, GLA_DV), 0.02),
        'w_out': nrm(ks[17], (DEPTH, MIX_WIDTH, D), MIX_WIDTH ** -0.5),
        'norm2_g': 1.0 + nrm(ks[18], (DEPTH, D), 0.02),
        'w_ff1': nrm(ks[19], (DEPTH, D, D_FF), D ** -0.5),
        'w_ff2': nrm(ks[20], (DEPTH, D_FF, D), D_FF ** -0.5),
        'final_norm_g': 1.0 + nrm(ks[21], (D,), 0.02),
    }


def reference(x, c, ctx, c_ctx, w_ada, b_ada, norm1_g, w_in, conv_w, conv_b, dt_bias, a_log,
              d_skip, ssd_norm_g, gla_w2, gla_b2, gla_norm_g, w_out, norm2_g, w_ff1, w_ff2,
              final_norm_g):
    f32 = jnp.float32
    bsz, n_lat, _ = x.shape
    rows = n_lat // GRID_W
    ctx_len = ctx.shape[1]
    zero_states = (jnp.zeros((bsz, SSD_GROUPS, SSD_HPG, SSD_HEAD_DIM, SSD_STATE), f32),
                   jnp.zeros((bsz, SSD_GROUPS, SSD_HPG, SSD_HEAD_DIM, SSD_STATE), f32),
                   jnp.zeros((bsz, GLA_HEADS, GLA_DK, GLA_DV), f32),
                   jnp.zeros((bsz, GLA_HEADS, GLA_DK, GLA_DV), f32))
    h_lat, h_ctx = x, ctx
    for l in range(DEPTH):
        mix_params = (w_in[l], conv_w[l], conv_b[l], dt_bias[l], a_log[l], d_skip[l],
                      ssd_norm_g[l], gla_w2[l], gla_b2[l], gla_norm_g[l])
        m_lat = jnp.split(jax.nn.silu(c) @ w_ada[l] + b_ada[l], 6, axis=-1)
        sh1, sc1, g1, sh2, sc2, g2 = [m[:, None, :] for m in m_lat]
        csh1, csc1, cg1, csh2, csc2, cg2 = jnp.split(jax.nn.silu(c_ctx) @ w_ada[l] + b_ada[l], 6, axis=-1)

        u_ctx = modulate(rms_norm(h_ctx, norm1_g[l]), csh1, csc1)
        heads_ctx, ctx_states = hybrid_mixer(u_ctx, 1, ctx_len, zero_states, *mix_params)

        u_lat = modulate(rms_norm(h_lat, norm1_g[l]), sh1, sc1)
        heads_lat, _ = hybrid_mixer(u_lat, rows, GRID_W, ctx_states, *mix_params)
        h_lat = h_lat + g1 * (heads_lat @ w_out[l])
        h_lat = h_lat + g2 * sq_relu_mlp(modulate(rms_norm(h_lat, norm2_g[l]), sh2, sc2), w_ff1[l], w_ff2[l])

        if l < DEPTH - 1:
            h_ctx = h_ctx + cg1 * (heads_ctx @ w_out[l])
            h_ctx = h_ctx + cg2 * sq_relu_mlp(modulate(rms_norm(h_ctx, norm2_g[l]), csh2, csc2), w_ff1[l], w_ff2[l])
    return rms_norm(h_lat, final_norm_g)
```

```python
import numpy as np
from collections import defaultdict
from contextlib import ExitStack
import concourse.bass as bass
import concourse.mybir as mybir
from concourse.bass_utils import run_bass_kernel_spmd

F32 = mybir.dt.float32
BF16 = mybir.dt.bfloat16
AF = mybir.ActivationFunctionType
ALU = mybir.AluOpType

D = 1024
KD = 8
CTX = 256
DEPTH = 2
EPS = 1e-6
ENG = ('sp', 'act', 'dve', 'pool', 'pe')
NQ = 8


class Buf:
    def __init__(self, nparts=1):
        self.n = nparts
        self.lw = [None] * nparts
        self.rd = [dict() for _ in range(nparts)]


def _expand(lst):
    for it in lst:
        if it is None:
            continue
        if isinstance(it, tuple):
            yield it
        elif isinstance(it, list):
            for x in _expand(it):
                yield x
        else:
            for i in range(it.n):
                yield (it, i)


class Sched:
    def __init__(self, nc):
        self.nc = nc
        self.h = {}
        for e in ('pe', 'act', 'dve', 'pool'):
            self.h[e] = nc.alloc_semaphore('s_' + e)
        for q in ('sp', 'pool', 'act'):
            for i in range(NQ):
                self.h[(q, i)] = nc.alloc_semaphore('d_%s%d' % (q, i))
        self.cnt = defaultdict(int)
        self.prog = {e: [] for e in ENG}
        self.waited = {e: {} for e in ENG}
        self.rr = defaultdict(int)
        self.nops = 0

    def _need(self, eng, deps):
        w = self.waited[eng]
        for k, v in deps.items():
            if w.get(k, 0) < v:
                self.prog[eng].append(('w', k, v))
                w[k] = v

    def _deps(self, reads, writes):
        deps = {}

        def add(ev):
            if ev is not None:
                k, v = ev
                if deps.get(k, 0) < v:
                    deps[k] = v
        for b, i in _expand(reads):
            add(b.lw[i])
        for b, i in _expand(writes):
            add(b.lw[i])
            for k, v in b.rd[i].items():
                add((k, v))
        return deps

    def _commit(self, ev, reads, writes):
        k, v = ev
        for b, i in _expand(reads):
            if b.rd[i].get(k, 0) < v:
                b.rd[i][k] = v
        for b, i in _expand(writes):
            b.lw[i] = ev
            b.rd[i] = {}

    def op(self, eng, fn, reads=(), writes=()):
        deps = self._deps(reads, writes)
        self._need(eng, deps)
        self.cnt[eng] += 1
        ev = (eng, self.cnt[eng])
        self.prog[eng].append(('o', fn, eng, 1))
        self._commit(ev, reads, writes)
        self.nops += 1

    def dma(self, q, out, in_, reads=(), writes=()):
        i = self.rr[q]
        self.rr[q] = (i + 1) % NQ
        key = (q, i)
        deps = self._deps(reads, writes)
        if self.cnt[key] > 0:
            deps[key] = max(deps.get(key, 0), self.cnt[key])
        self._need(q, deps)
        self.cnt[key] += 16
        ev = (key, self.cnt[key])
        self.prog[q].append(('o', (lambda e, o=out, s=in_: e.dma_start(out=o, in_=s)), key, 16))
        self._commit(ev, reads, writes)
        self.nops += 1

    def flush(self):
        final = {k: v for k, v in self.cnt.items() if v > 0}
        for e in ENG:
            self._need(e, final)
        progs = self.prog
        self.prog = {e: [] for e in ENG}
        hmap = self.h
        with self.nc.Block() as block:
            def mk(prog):
                def body(eng):
                    for item in prog:
                        if item[0] == 'w':
                            eng.wait_ge(hmap[item[1]], item[2])
                        else:
                            ins = item[1](eng)
                            ins.then_inc(hmap[item[2]], item[3])
                return body
            block.sync(mk(progs['sp']))
            block.scalar(mk(progs['act']))
            block.vector(mk(progs['dve']))
            block.gpsimd(mk(progs['pool']))
            block.tensor(mk(progs['pe']))


def MM(lst):
    def f(e):
        ins = None
        for (o, l, r, s, t) in lst:
            ins = e.matmul(o, l, r, start=s, stop=t)
        return ins
    return f


def TR(lst):
    def f(e):
        ins = None
        for (o, i, idn) in lst:
            ins = e.transpose(o, i, idn)
        return ins
    return f


def ACT(out, in_, func, **kw):
    return lambda e: e.activation(out=out, in_=in_, func=func, **kw)


def TT(out, in0, in1, op):
    return lambda e: e.tensor_tensor(out=out, in0=in0, in1=in1, op=op)


def TS(out, in0, s1, s2, op0, op1=None):
    if op1 is None:
        return lambda e: e.tensor_scalar(out=out, in0=in0, scalar1=s1, scalar2=None, op0=op0)
    return lambda e: e.tensor_scalar(out=out, in0=in0, scalar1=s1, scalar2=s2, op0=op0, op1=op1)


def STT(out, in0, scalar, in1, op0, op1):
    return lambda e: e.scalar_tensor_tensor(out=out, in0=in0, scalar=scalar, in1=in1, op0=op0, op1=op1)


def CP(out, in_):
    return lambda e: e.tensor_copy(out=out, in_=in_)


def MS(out, val):
    return lambda e: e.memset(out, val)


def RECIP(out, in_):
    return lambda e: e.reciprocal(out=out, in_=in_)


_UID = [0]


class Pool:
    def __init__(self, es, nc, name, shape, dtype, n, nparts=1):
        _UID[0] += 1
        self.t = es.enter_context(nc.sbuf_tensor("%s_%d" % (name, _UID[0]), [shape[0], n] + list(shape[1:]), dtype))
        self.n = n
        self.bufs = [Buf(nparts) for _ in range(n)]
        self.i = 0

    def next(self):
        i = self.i
        self.i = (i + 1) % self.n
        return self.t[:, i], self.bufs[i]


def single(es, nc, name, shape, dtype, nparts=1):
    _UID[0] += 1
    t = es.enter_context(nc.sbuf_tensor("%s_%d" % (name, _UID[0]), list(shape), dtype))
    return t, Buf(nparts)


def build(L, debug=False, nlayers=DEPTH, stop_after=None):
    T = CTX + L
    NCH = T // 128
    TP = T + 384
    nc = bass.Bass("TRN2", target_bir_lowering=False)
    S = Sched(nc)
    okind = "ExternalOutput" if debug else "Internal"

    def din(name, shape, dt=F32):
        return nc.dram_tensor(name, list(shape), dt, kind="ExternalInput").ap()

    def dscr(name, shape, dt):
        return nc.dram_tensor(name, list(shape), dt, kind=okind).ap()

    x_in = din("x", [L, D])
    ctx_in = din("ctx", [CTX, D])
    ccol_in = din("ccol", [128, KD, 2])
    w_ada = din("w_ada", [DEPTH, D, 6 * D])
    bada_row = din("bada_row", [DEPTH, 1, 6 * D])
    bada_col = din("bada_col", [DEPTH, 128, 48])
    n1g_col = din("n1g_col", [DEPTH, 128, KD])
    n2g_col = din("n2g_col", [DEPTH, 128, KD])
    w_in = din("w_in", [DEPTH, D, 5696])
    convw_col = din("convw_col", [DEPTH, 128, 12, 9])
    convb_col = din("convb_col", [DEPTH, 128, 12])
    convb_row = din("convb_row", [DEPTH, 1, 1536])
    dtb_in = din("dt_bias", [DEPTH, 32])
    alog_in = din("a_log", [DEPTH, 32])
    dskip_in = din("d_skip", [DEPTH, 32])
    ssdg_col_in = din("ssdg_col", [DEPTH, 128, 8])
    glag_col_in = din("glag_col", [DEPTH, 128, 2])
    w2aug_in = din("w2aug", [DEPTH, 2, 17, 512])
    w_out = din("w_out", [DEPTH, 2 * D, D])
    w_ff1 = din("w_ff1", [DEPTH, D, 4 * D])
    w_ff2 = din("w_ff2", [DEPTH, 4 * D, D])
    fng_in = din("final_norm_g", [D])
    out = nc.dram_tensor("out", [L, D], F32, kind="ExternalOutput").ap()

    H1 = dscr("H1", [T, D], F32)
    H2 = dscr("H2", [T, D], F32)
    ZS = dscr("ZS", [T, D], BF16)
    RS = dscr("RS", [T, D], BF16)
    V = dscr("V", [T, D], BF16)
    KTM = dscr("KTM", [T, 512], BF16)
    XBCT = dscr("XBCT", [1536, TP], BF16)
    QKB = dscr("QKB", [NCH, 128, 8, 128], BF16)
    GAT = dscr("GAT", [2, 17, T], F32)
    XS = dscr("XS", [T, D], BF16)
    BTM = dscr("BTM", [T, 256], BF16)
    BCB = dscr("BCB", [NCH, 128, 4, 128], BF16)
    GBC = dscr("GBC", [2, 2, D], F32)
    YF = dscr("YF", [T, D], F32)
    OF = dscr("OF", [T, D], F32)
    dram_bufs = {}

    def DB(name, n=NCH):
        if name not in dram_bufs:
            dram_bufs[name] = Buf(n)
        return dram_bufs[name]

    def xcol(t):
        return t + 128 if t < CTX else t + 256

    tiles = [(0, CTX)] + [(CTX + i * 512, 512) for i in range(L // 512)]

    with ExitStack() as ges:
        ps = ges.enter_context(nc.psum_tensor("ps", [128, 8, 512], F32))
        pbufs = [Buf() for _ in range(8)]
        pst = {'i': 0}

        def palloc(n=1):
            i = pst['i']
            if n > 1:
                i = ((i + n - 1) // n) * n
            if i + n > 8:
                i = 0
            pst['i'] = (i + n) % 8
            ap = ps[:, i:i + n, :].rearrange("p b f -> p (b f)") if n > 1 else ps[:, i, :]
            return ap, pbufs[i:i + n]

        def load_cast(stage_pool, dst, src, dstb, i, q='sp'):
            n = dst.shape[-1]
            step = stage_pool.t.shape[-1]
            for o in range(0, n, step):
                m = min(step, n - o)
                st_, stb_ = stage_pool.next()
                S.dma(q, st_[:, 0:m], src[:, o:o + m], writes=[stb_])
                eng = ('dve', 'act', 'pool')[(i + o // step) % 3]
                if eng == 'act':
                    S.op('act', ACT(dst[:, o:o + m], st_[:, 0:m], AF.Copy), reads=[stb_], writes=[dstb])
                else:
                    S.op(eng, CP(dst[:, o:o + m], st_[:, 0:m]), reads=[stb_], writes=[dstb])

        ident, ident_b = single(ges, nc, "ident", [128, 128], BF16)
        Umat, U_b = single(ges, nc, "Umat", [128, 4, 128], F32)
        ones32, ones32_b = single(ges, nc, "ones32", [128, 128], F32)
        ones16, ones16_b = single(ges, nc, "ones16", [128, 128], BF16)
        S.op('pool', MS(ident[:], 0.0), writes=[ident_b])
        S.op('pool', lambda e: e.affine_select(out=ident[:], in_=ident[:], pattern=[[-1, 128]], compare_op=ALU.not_equal,
                                                fill=1.0, base=0, channel_multiplier=1), reads=[ident_b], writes=[ident_b])
        S.op('pool', MS(Umat[:], 1.0), writes=[U_b])
        for mi, (pat, cm, cop) in enumerate([(1, -1, ALU.is_ge), (-1, 1, ALU.is_ge), (-1, 1, ALU.is_gt), (1, -1, ALU.is_gt)]):
            S.op('pool', (lambda e, mi=mi, pat=pat, cm=cm, cop=cop: e.affine_select(
                out=Umat[:, mi, :], in_=Umat[:, mi, :], pattern=[[pat, 128]], compare_op=cop, fill=0.0, base=0,
                channel_multiplier=cm)), reads=[U_b], writes=[U_b])
        U16, U16_b = single(ges, nc, "U16", [128, 4, 128], BF16)
        S.op('pool', CP(U16[:], Umat[:]), reads=[U_b], writes=[U16_b])
        S.op('pool', MS(ones32[:], 1.0), writes=[ones32_b])
        S.op('pool', MS(ones16[:], 1.0), writes=[ones16_b])
        xb3 = XBCT.rearrange("(cc p) t -> p cc t", p=128)

        ssb, ss_b = single(ges, nc, "s_sb", [128, KD, 2], F32)
        modc, modc_b = single(ges, nc, "modc", [128, 48, 2], F32)
        A1c, A1_b = single(ges, nc, "A1c", [128, KD, 2], F32)
        A2c, A2_b = single(ges, nc, "A2c", [128, KD, 2], F32)
        eps_t, eps_b = single(ges, nc, "eps_t", [128, 1], F32)
        S.op('pool', MS(eps_t[:], EPS), writes=[eps_b])
        S.dma('sp', ssb[:], ccol_in, writes=[ss_b])
        S.op('act', ACT(ssb[:], ssb[:], AF.Silu), reads=[ss_b], writes=[ss_b])
        with ExitStack() as pes:
            zero16, zero16_b = single(pes, nc, "zero16", [128, 12, 128], BF16)
            S.op('pool', MS(zero16[:], 0.0), writes=[zero16_b])
            for pad0 in (0, 128 + CTX, TP - 128):
                S.dma('sp', xb3[:, :, pad0:pad0 + 128], zero16[:], reads=[zero16_b], writes=[DB('XBCT', 1)])
            S.flush()

        for l in range(nlayers):
            last = (l == DEPTH - 1)

            def h_src(c):
                if l == 0:
                    if c < 2:
                        return ctx_in[c * 128:(c + 1) * 128, :]
                    return x_in[(c - 2) * 128:(c - 1) * 128, :]
                return H2[c * 128:(c + 1) * 128, :]

            les = ExitStack()
            dt_all, dt_b = single(les, nc, "dt_all", [128, NCH, 32], F32, NCH)
            a_all, a_b = single(les, nc, "a_all", [128, NCH, 32], F32, NCH)
            ahl_all, ahl_b = single(les, nc, "ahl_all", [128, NCH, 2, 32], BF16, NCH)
            wes = ExitStack()
            wtm, wtm_b = single(wes, nc, "wtm", [128, KD, 3616], BF16)
            wfm, wfm_b = single(wes, nc, "wfm", [128, KD, 2592], BF16)
            w_in3 = w_in[l].rearrange("(kc p) n -> p kc n", p=128)
            wstA_pool = Pool(wes, nc, "wstA", [128, 1024], F32, 2)
            wi = 0
            for kc in range(KD):
                for (d0, s0, n) in [(0, 0, 1024), (1024, 3104, 512), (1536, 3616, 1024), (2560, 4640, 1024), (3584, 2560, 32)]:
                    load_cast(wstA_pool, wtm[:, kc, d0:d0 + n], w_in3[:, kc, s0:s0 + n], wtm_b, wi, q='act')
                    wi += 1
                for (d0, s0, n) in [(0, 1024, 1536), (1536, 2592, 1024), (2560, 5664, 32)]:
                    load_cast(wstA_pool, wfm[:, kc, d0:d0 + n], w_in3[:, kc, s0:s0 + n], wfm_b, wi, q='act')
                    wi += 1

            with ExitStack() as es:
                wa_pool = Pool(es, nc, "wa", [128, KD, 512], F32, 3)
                brow, brow_b = single(es, nc, "brow", [1, 2 * D], F32)
                bcol, bcol_b = single(es, nc, "bcol", [128, 48], F32)
                n1c, n1c_b = single(es, nc, "n1c", [128, KD], F32)
                n2c, n2c_b = single(es, nc, "n2c", [128, KD], F32)
                srep, srep_b = single(es, nc, "srep", [128, KD, 2, 128], F32)
                S.op('dve', CP(srep[:], ssb[:].unsqueeze(3).to_broadcast([128, KD, 2, 128])), reads=[ss_b], writes=[srep_b])
                grow_pool = Pool(es, nc, "grow", [128, 512], F32, 2)
                S.dma('sp', brow[0:1, 0:D], bada_row[l][0:1, 2 * D:3 * D], writes=[brow_b])
                S.dma('sp', brow[0:1, D:2 * D], bada_row[l][0:1, 5 * D:6 * D], writes=[brow_b])
                S.dma('sp', bcol[:], bada_col[l], writes=[bcol_b])
                S.dma('sp', n1c[:], n1g_col[l], writes=[n1c_b])
                S.dma('sp', n2c[:], n2g_col[l], writes=[n2c_b])
                for j in (0, 1, 2, 3, 6, 7, 8, 9, 4, 5, 10, 11):
                    wa, wab = wa_pool.next()
                    S.dma('sp', wa, w_ada[l][:, j * 512:(j + 1) * 512].rearrange("(kc p) n -> p kc n", p=128), writes=[wab])
                    idx, half = j // 2, j % 2
                    if idx in (2, 5):
                        for s in range(2):
                            pa, pb = palloc()
                            lst = [(pa, srep[:, kc, s, :], wa[:, kc, :], kc == 0, False) for kc in range(KD)]
                            gi_ = 0 if idx == 2 else 1
                            lst.append((pa, ones32[0:1, :], brow[0:1, gi_ * D + half * 512:gi_ * D + (half + 1) * 512], False, True))
                            S.op('pe', MM(lst), reads=[wab, srep_b, brow_b, ones32_b], writes=pb)
                            gi = 0 if idx == 2 else 1
                            gr, grb = grow_pool.next()
                            S.op('dve', CP(gr, pa), reads=pb, writes=[grb])
                            S.dma('sp', GBC[gi, s:s + 1, half * 512:(half + 1) * 512], gr[0:1, :], reads=[grb], writes=[DB('GBC', 1)])
                    else:
                        pa, pb = palloc()
                        lst = []
                        for n4 in range(4):
                            for kc in range(KD):
                                lst.append((pa[:, n4 * 2:n4 * 2 + 2], wa[:, kc, n4 * 128:(n4 + 1) * 128], ssb[:, kc, :], kc == 0, kc == KD - 1))
                        S.op('pe', MM(lst), reads=[wab, ss_b], writes=pb)
                        S.op('dve', TT(modc[:, j * 4:(j + 1) * 4, :], pa[:, 0:8].rearrange("p (a b) -> p a b", b=2),
                                       bcol[:, j * 4:(j + 1) * 4].unsqueeze(2).to_broadcast([128, 4, 2]), ALU.add),
                             reads=pb + [bcol_b], writes=[modc_b])
                S.op('dve', STT(A1c[:], modc[:, 8:16, :], 1.0, n1c[:].unsqueeze(2).to_broadcast([128, KD, 2]), ALU.add, ALU.mult),
                     reads=[modc_b, n1c_b], writes=[A1_b])
                S.op('dve', STT(A2c[:], modc[:, 32:40, :], 1.0, n2c[:].unsqueeze(2).to_broadcast([128, KD, 2]), ALU.add, ALU.mult),
                     reads=[modc_b, n2c_b], writes=[A2_b])
                S.flush()

            def norm_T(pools, ci, hx, hxb, uT, uTb, Ac, Ab, shoff, s):
                xn, xnb = norm_pre(pools, hx, hxb)
                norm_pe(xn, xnb, ci, uT, uTb, Ac, Ab, shoff, s)

            def norm_pre(pools, hx, hxb):
                ssp, xnp = pools
                ss, ssb_ = ssp.next()
                xn, xnb = xnp.next()
                S.op('dve', MS(ss[:, 0:1], 0.0), writes=[ssb_])
                S.op('act', ACT(xn, hx, AF.Square, accum_out=ss[:, 0:1]), reads=[hxb, ssb_], writes=[xnb, ssb_])
                S.op('act', ACT(ss[:, 1:2], ss[:, 0:1], AF.Sqrt, scale=1.0 / D, bias=eps_t[:, 0:1]), reads=[ssb_, eps_b], writes=[ssb_])
                S.op('dve', RECIP(ss[:, 2:3], ss[:, 1:2]), reads=[ssb_], writes=[ssb_])
                S.op('dve', TS(xn, hx, ss[:, 2:3], None, ALU.mult), reads=[hxb, ssb_], writes=[xnb])
                return xn, xnb

            def norm_pe(xn, xnb, ci, uT, uTb, Ac, Ab, shoff, s):
                pa, pb = palloc()
                pT = pa.bitcast(BF16)
                S.op('pe', TR([(pT[:, kc * 128:(kc + 1) * 128], xn[:, kc * 128:(kc + 1) * 128], ident[:]) for kc in range(KD)]),
                     reads=[xnb, ident_b], writes=pb)
                for kc in range(KD):
                    o = uT[:, kc, ci * 128:(ci + 1) * 128]
                    i_ = pT[:, kc * 128:(kc + 1) * 128]
                    sc_ = Ac[:, kc, s:s + 1]
                    bi_ = modc[:, shoff + kc, s:s + 1]
                    if kc % 2 == 0:
                        S.op('act', ACT(o, i_, AF.Identity, scale=sc_, bias=bi_), reads=pb + [Ab, modc_b], writes=[(uTb, ci)])
                    else:
                        S.op('dve', TS(o, i_, sc_, bi_, ALU.mult, ALU.add), reads=pb + [Ab, modc_b], writes=[(uTb, ci)])

            with ExitStack() as es:
                dtb_bc, dtb_b = single(es, nc, "dtb_bc", [128, 32], F32)
                A_bc, A_b = single(es, nc, "A_bc", [128, 32], F32)
                S.dma('sp', dtb_bc[:], dtb_in[l].partition_broadcast(128), writes=[dtb_b])
                S.dma('sp', A_bc[:], alog_in[l].partition_broadcast(128), writes=[A_b])
                S.op('act', ACT(A_bc[:], A_bc[:], AF.Exp), reads=[A_b], writes=[A_b])
                S.op('dve', TS(A_bc[:], A_bc[:], -1.0, None, ALU.mult), reads=[A_b], writes=[A_b])
                hx_pool = Pool(es, nc, "hxA", [128, D], F32, 3)
                ssp = Pool(es, nc, "ssA", [128, 4], F32, 4)
                xnp = Pool(es, nc, "xnA", [128, D], BF16, 4)
                uT_pool = Pool(es, nc, "uTA", [128, KD, 512], BF16, 2, 4)
                zs_pool = Pool(es, nc, "zsA", [128, D], BF16, 2, 2)
                rs_pool = Pool(es, nc, "rsA", [128, D], BF16, 2, 2)
                v_pool = Pool(es, nc, "vA", [128, D], BF16, 2, 2)
                k_pool = Pool(es, nc, "kA", [128, 512], BF16, 2)
                dtmp_pool = Pool(es, nc, "dtmpA", [128, 32], F32, 2)
                xbc_pool = Pool(es, nc, "xbcA", [128, 12, 512], BF16, 1, 12)
                qk_pool = Pool(es, nc, "qkA", [128, 4, 8, 128], BF16, 1, 8)
                ga_pool = Pool(es, nc, "gaA", [17, 2, 512], F32, 1, 2)
                for _ in range(1):
                    ga, gab = ga_pool.next()
                    S.op('pool', MS(ga, 1.0), writes=[gab])
                gat3 = GAT.rearrange("d r t -> r d t")
                evi = 0
                def preA(tok0, ntok):
                    xns = []
                    for ci in range(ntok // 128):
                        c = tok0 // 128 + ci
                        hx, hxb = hx_pool.next()
                        S.dma('sp', hx, h_src(c), reads=[(DB('H2'), c)] if l > 0 else [], writes=[hxb])
                        xns.append(norm_pre((ssp, xnp), hx, hxb))
                    return xns

                def peA(xns, tok0):
                    uT_, uTb_ = uT_pool.next()
                    s_ = 1 if tok0 < CTX else 0
                    for ci, (xn, xnb) in enumerate(xns):
                        norm_pe(xn, xnb, ci, uT_, uTb_, A1c, A1_b, 0, s_)
                    return uT_, uTb_

                uTq = {0: peA(preA(*tiles[0]), tiles[0][0])}
                for ti, (tok0, ntok) in enumerate(tiles):
                    nchk = ntok // 128
                    s = 1 if tok0 < CTX else 0
                    uT, uTb = uTq.pop(ti)
                    xns_next = preA(*tiles[ti + 1]) if ti + 1 < len(tiles) else None
                    for ci in range(nchk):
                        c = tok0 // 128 + ci
                        rows = slice(c * 128, (c + 1) * 128)
                        lhs = [uT[:, kc, ci * 128:(ci + 1) * 128] for kc in range(KD)]
                        zs, zsb = zs_pool.next()
                        rs, rsb = rs_pool.next()
                        vv, vvb = v_pool.next()
                        kk, kkb = k_pool.next()
                        for (w0, n, kind, half) in [(0, 512, 'z', 0), (512, 512, 'z', 1), (1024, 512, 'k', 0), (1536, 512, 'v', 0),
                                                    (2048, 512, 'v', 1), (2560, 512, 'r', 0), (3072, 512, 'r', 1), (3584, 32, 'dt', 0)]:
                            pa, pb = palloc()
                            S.op('pe', MM([(pa[:, 0:n], lhs[kc], wtm[:, kc, w0:w0 + n], kc == 0, kc == KD - 1) for kc in range(KD)]),
                                 reads=[(uTb, ci), wtm_b], writes=pb)
                            hs = slice(half * 512, half * 512 + 512)
                            if kind == 'z':
                                S.op('act', ACT(zs[:, hs], pa, AF.Silu), reads=pb, writes=[(zsb, half)])
                            elif kind == 'r':
                                S.op('act', ACT(rs[:, hs], pa, AF.Silu), reads=pb, writes=[(rsb, half)])
                            elif kind == 'v':
                                S.op('dve', CP(vv[:, hs], pa), reads=pb, writes=[(vvb, half)])
                            elif kind == 'k':
                                S.op('dve', CP(kk, pa), reads=pb, writes=[kkb])
                            else:
                                dtm, dtmb = dtmp_pool.next()
                                S.op('dve', TT(dtm, pa[:, 0:32], dtb_bc[:], ALU.add), reads=pb + [dtb_b], writes=[dtmb])
                                S.op('act', ACT(dtm, dtm, AF.Exp), reads=[dtmb], writes=[dtmb])
                                S.op('act', ACT(dt_all[:, c, :], dtm, AF.Ln, bias=1.0), reads=[dtmb], writes=[(dt_b, c)])
                                S.op('dve', TT(a_all[:, c, :], dt_all[:, c, :], A_bc[:], ALU.mult), reads=[(dt_b, c), A_b], writes=[(a_b, c)])
                                S.op('dve', CP(ahl_all[:, c, 0, :], a_all[:, c, :]), reads=[(a_b, c)], writes=[(ahl_b, c)])
                                S.op('dve', TT(ahl_all[:, c, 1, :], a_all[:, c, :], ahl_all[:, c, 0, :], ALU.subtract), reads=[(a_b, c), (ahl_b, c)], writes=[(ahl_b, c)])
                        S.dma('sp', ZS[rows, :], zs, reads=[zsb], writes=[(DB('ZS'), c)])
                        S.dma('sp', RS[rows, :], rs, reads=[rsb], writes=[(DB('RS'), c)])
                        S.dma('sp', V[rows, :], vv, reads=[vvb], writes=[(DB('V'), c)])
                        S.dma('sp', KTM[rows, :], kk, reads=[kkb], writes=[(DB('KTM'), c)])
                    if xns_next is not None:
                        uTq[ti + 1] = peA(xns_next, tiles[ti + 1][0])
                    xbc, xbcb = xbc_pool.next()
                    qk, qkb = qk_pool.next()
                    ga, gab = ga_pool.next()
                    rhs = [uT[:, kc, 0:ntok] for kc in range(KD)]
                    for j in range(20):
                        pa, pb = palloc()
                        S.op('pe', MM([(pa[:, 0:ntok], wfm[:, kc, j * 128:(j + 1) * 128], rhs[kc], kc == 0, kc == KD - 1) for kc in range(KD)]),
                             reads=[uTb, wfm_b], writes=pb)
                        eng = 'act' if (evi % 2 == 0) else 'dve'
                        evi += 1
                        if j < 12:
                            dst, dstb = xbc[:, j, 0:ntok], (xbcb, j)
                        else:
                            dst, dstb = qk[:, 0:nchk, j - 12, :], (qkb, j - 12)
                        scale = (128.0 ** -0.5) if (12 <= j < 16) else 1.0
                        src_ = pa[:, 0:ntok] if j < 12 else pa[:, 0:ntok].rearrange("p (c t) -> p c t", t=128)
                        if eng == 'act':
                            S.op('act', ACT(dst, src_, AF.Copy, scale=scale), reads=pb, writes=[dstb])
                        else:
                            S.op('dve', TS(dst, src_, scale, None, ALU.mult), reads=pb, writes=[dstb])
                    for dd in range(2):
                        pa, pb = palloc()
                        S.op('pe', MM([(pa[0:16, 0:ntok], wfm[:, kc, 2560 + dd * 16:2576 + dd * 16], rhs[kc], kc == 0, kc == KD - 1) for kc in range(KD)]),
                             reads=[uTb, wfm_b], writes=pb)
                        S.op('dve', CP(ga[0:16, dd, 0:ntok], pa[0:16, 0:ntok]), reads=pb, writes=[(gab, dd)])
                    c0 = xcol(tok0)
                    ch = list(range(tok0 // 128, tok0 // 128 + nchk))
                    S.dma('sp', xb3[:, :, c0:c0 + ntok], xbc[:, :, 0:ntok], reads=[xbcb], writes=[DB('XBCT', 1)])
                    for ci, c in enumerate(ch):
                        S.dma('sp', QKB[c], qk[:, ci], reads=[qkb], writes=[(DB('QKB'), c)])
                    S.dma('sp', gat3[:, :, tok0:tok0 + ntok], ga[:, :, 0:ntok], reads=[gab], writes=[(DB('GAT'), c) for c in ch])
                S.flush()
            wes.close()
            if stop_after == 'A':
                les.close()
                break

            with ExitStack() as es:
                dg, dg_b = single(es, nc, "dg", [128, 12, 9, 128], BF16)
                cwc, cwc_b = single(es, nc, "cwc", [128, 12, 9], F32)
                cbc, cbc_b = single(es, nc, "cbc", [128, 12], F32)
                S.dma('sp', cwc[:], convw_col[l], writes=[cwc_b])
                S.dma('sp', cbc[:], convb_col[l], writes=[cbc_b])
                for cc in range(12):
                    S.op('dve' if cc % 2 == 0 else 'pool', TT(dg[:, cc], ident[:].unsqueeze(1).to_broadcast([128, 9, 128]),
                                   cwc[:, cc, :].unsqueeze(2).to_broadcast([128, 9, 128]), ALU.mult),
                         reads=[ident_b, cwc_b], writes=[dg_b])
                xw_pool = Pool(es, nc, "xwB", [128, 12, 768], BF16, 2)
                xl_pool = Pool(es, nc, "xlB", [128, 12, 768], BF16, 2)
                xr_pool = Pool(es, nc, "xrB", [128, 12, 768], BF16, 2)
                fm_pool = Pool(es, nc, "fmB", [128, 10, 512], BF16, 2, 10)
                xs_pool = Pool(es, nc, "xsB", [128, D], BF16, 3)
                bt_pool = Pool(es, nc, "btB", [128, 256], BF16, 3)
                bc_pool = Pool(es, nc, "bcB", [128, 4, 4, 128], BF16, 2, 4)
                def prepB(tok0, ntok):
                    nchk = ntok // 128
                    isctx = tok0 < CTX
                    c0 = xcol(tok0) - 128
                    W = ntok + 256
                    xw, xwb = xw_pool.next()
                    S.dma('sp', xw[:, :, 0:W], xb3[:, :, c0:c0 + W], reads=[DB('XBCT', 1)], writes=[xwb])
                    if isctx:
                        taps = [(3 + dc + 1, xw, xwb, 128 + dc) for dc in (-1, 0, 1)]
                    else:
                        xl, xlb = xl_pool.next()
                        xr, xrb = xr_pool.next()
                        S.op('act', ACT(xl[:, :, 0:W], xw[:, :, 0:W], AF.Copy), reads=[xwb], writes=[xlb])
                        S.op('dve', CP(xr[:, :, 0:W], xw[:, :, 0:W]), reads=[xwb], writes=[xrb])
                        S.op('pool', MS(xl[:, :, 63:W:64], 0.0), reads=[xlb], writes=[xlb])
                        S.op('pool', MS(xr[:, :, 0:W:64], 0.0), reads=[xrb], writes=[xrb])
                        taps = []
                        for dr in (-1, 0, 1):
                            for dc in (-1, 0, 1):
                                src, srcb = (xl, xlb) if dc == -1 else ((xr, xrb) if dc == 1 else (xw, xwb))
                                taps.append(((dr + 1) * 3 + dc + 1, src, srcb, 128 + 64 * dr + dc))
                    return taps

                tapsq = {0: prepB(*tiles[0])}
                for ti, (tok0, ntok) in enumerate(tiles):
                    nchk = ntok // 128
                    if ti + 1 < len(tiles):
                        tapsq[ti + 1] = prepB(*tiles[ti + 1])
                    taps = tapsq.pop(ti)
                    tbufs = [t[2] for t in taps]
                    fm, fmb = fm_pool.next()
                    bc, bcb = bc_pool.next()
                    for cc in range(12):
                        pa, pb = palloc()
                        lst = [(pa[:, 0:ntok], dg[:, cc, k, :], src[:, cc, off:off + ntok], ti == 0, ti == len(taps) - 1)
                               for ti, (k, src, srcb, off) in enumerate(taps)]
                        S.op('pe', MM(lst), reads=tbufs + [dg_b], writes=pb)
                        if cc < 10:
                            S.op('act', ACT(fm[:, cc, 0:ntok], pa[:, 0:ntok], AF.Silu, bias=cbc[:, cc:cc + 1]), reads=pb + [cbc_b], writes=[(fmb, cc)])
                        if cc >= 8:
                            c4 = cc - 8
                            if cc < 10:
                                S.op('dve', CP(bc[:, 0:nchk, c4, :], fm[:, cc, 0:ntok].rearrange("p (c t) -> p c t", t=128)),
                                     reads=[(fmb, cc)], writes=[(bcb, c4)])
                            else:
                                S.op('act', ACT(bc[:, 0:nchk, c4, :], pa[:, 0:ntok].rearrange("p (c t) -> p c t", t=128), AF.Silu, bias=cbc[:, cc:cc + 1]),
                                     reads=pb + [cbc_b], writes=[(bcb, c4)])
                    ch = list(range(tok0 // 128, tok0 // 128 + nchk))
                    for ci, c in enumerate(ch):
                        S.dma('sp', BCB[c], bc[:, ci], reads=[bcb], writes=[(DB('BCB'), c)])
                    for ci, c in enumerate(ch):
                        rows = slice(c * 128, (c + 1) * 128)
                        pa, pb = palloc()
                        pT = pa.bitcast(BF16)
                        S.op('pe', TR([(pT[:, cc * 128:(cc + 1) * 128], fm[:, cc, ci * 128:(ci + 1) * 128], ident[:]) for cc in range(8)]),
                             reads=[(fmb, cc) for cc in range(8)] + [ident_b], writes=pb)
                        xs, xsb = xs_pool.next()
                        S.op('dve', CP(xs, pT), reads=pb, writes=[xsb])
                        S.dma('sp', XS[rows, :], xs, reads=[xsb], writes=[(DB('XS'), c)])
                        pa, pb = palloc()
                        pT = pa.bitcast(BF16)
                        S.op('pe', TR([(pT[:, c2 * 128:(c2 + 1) * 128], fm[:, 8 + c2, ci * 128:(ci + 1) * 128], ident[:]) for c2 in range(2)]),
                             reads=[(fmb, 8), (fmb, 9), ident_b], writes=pb)
                        bt, btb = bt_pool.next()
                        S.op('dve', CP(bt, pT[:, 0:256]), reads=pb, writes=[btb])
                        S.dma('sp', BTM[rows, :], bt, reads=[btb], writes=[(DB('BTM'), c)])
                S.flush()
            if stop_after == 'B':
                les.close()
                break

            with ExitStack() as es:
                w2a, w2a_b = single(es, nc, "w2a", [17, 2, 512], F32)
                S.dma('sp', w2a[:], w2aug_in[l].rearrange("d r n -> r d n"), writes=[w2a_b])
                D_bc, D_b = single(es, nc, "D_bc", [128, 32], F32)
                S.dma('sp', D_bc[:], dskip_in[l].partition_broadcast(128), writes=[D_b])
                Dd, Dd_b = single(es, nc, "Dd", [128, 2, 32, 128], BF16)
                Dhl, Dhl_b = single(es, nc, "Dhl", [128, 2, 32], F32)
                Dh16, Dh16_b = single(es, nc, "Dh16", [128, 32], BF16)
                S.op('dve', CP(Dh16[:], D_bc[:]), reads=[D_b], writes=[Dh16_b])
                S.op('dve', CP(Dhl[:, 0, :], Dh16[:]), reads=[Dh16_b], writes=[Dhl_b])
                S.op('dve', TT(Dhl[:, 1, :], D_bc[:], Dhl[:, 0, :], ALU.subtract), reads=[D_b, Dhl_b], writes=[Dhl_b])
                for hl in range(2):
                    S.op('dve', TT(Dd[:, hl], ident[:].unsqueeze(1).to_broadcast([128, 32, 128]),
                                   Dhl[:, hl, :].unsqueeze(2).to_broadcast([128, 32, 128]), ALU.mult), reads=[ident_b, Dhl_b], writes=[Dd_b])
                pyc_pool = Pool(es, nc, "pycC", [128, D], F32, 2)
                hst, hst_b = single(es, nc, "hst", [128, D], F32)
                hbf, hbf_b = single(es, nc, "hbf", [128, D], BF16)
                Sst, Sst_b = single(es, nc, "Sst", [128, D], F32, 4)
                Sbf, Sbf_b = single(es, nc, "Sbf", [128, D], BF16)
                xs_pool = Pool(es, nc, "xsC", [128, D], BF16, 4)
                btm_pool = Pool(es, nc, "btmC", [128, 256], BF16, 4)
                bct_pool = Pool(es, nc, "bctC", [128, 4, 128], BF16, 4)
                qk_pool = Pool(es, nc, "qkC", [128, 8, 128], BF16, 4)
                ktm_pool = Pool(es, nc, "ktmC", [128, 512], BF16, 3)
                v_pool = Pool(es, nc, "vC", [128, D], BF16, 4)
                ga_pool = Pool(es, nc, "gaC", [17, 128], F32, 3)
                yf_pool = Pool(es, nc, "yfC", [128, D], F32, 3)
                of_pool = Pool(es, nc, "ofC", [128, D], F32, 3)
                sm_pool = Pool(es, nc, "smC", [128, 48], F32, 3)
                rb_pool = Pool(es, nc, "rbC", [128, 2048], F32, 2)
                E_pool = Pool(es, nc, "EC", [128, 2048], BF16, 1)
                cbm_pool = Pool(es, nc, "cbmC", [128, 256], BF16, 2)
                MT_pool = Pool(es, nc, "MTC", [128, 2048], BF16, 2)
                xdt_pool = Pool(es, nc, "xdtC", [128, D], BF16, 2)
                xdd_pool = Pool(es, nc, "xddC", [128, D], BF16, 2)
                yt_pool = Pool(es, nc, "ytC", [128, D], F32, 1)
                ys_pool = Pool(es, nc, "ysC", [128, D], F32, 2)
                sp_pool = Pool(es, nc, "spC", [128, 512], F32, 1)
                sph_pool = Pool(es, nc, "sphC", [128, 2, 512], BF16, 2)
                eq_pool = Pool(es, nc, "eqC", [128, 512], F32, 2)
                ek_pool = Pool(es, nc, "ekC", [128, 512], F32, 1)
                kes_pool = Pool(es, nc, "kesC", [128, 512], F32, 1)
                qd_pool = Pool(es, nc, "qdC", [128, 512], BF16, 2)
                ki_pool = Pool(es, nc, "kiC", [128, 512], BF16, 2)
                ke_pool = Pool(es, nc, "keC", [128, 512], BF16, 2)
                att_pool = Pool(es, nc, "attC", [128, 512], BF16, 2)
                os_pool = Pool(es, nc, "osC", [128, D], F32, 2)

                def b3(ap, a, b_):
                    return ap.rearrange("p (a b) -> p a b", a=a, b=b_)

                def ld(c, dirn, need_out, pass2):
                    X = {'c': c, 'dirn': dirn, 'need': need_out}
                    rows = slice(c * 128, (c + 1) * 128)
                    cols = slice(c * 128, (c + 1) * 128)
                    xs, xsb = xs_pool.next()
                    S.dma('sp', xs, XS[rows, :], reads=[(DB('XS'), c)], writes=[xsb])
                    btm, btmb = btm_pool.next()
                    S.dma('sp', btm, BTM[rows, :], reads=[(DB('BTM'), c)], writes=[btmb])
                    bct, bctb = bct_pool.next()
                    S.dma('sp', bct, BCB[c], reads=[(DB('BCB'), c)], writes=[bctb])
                    qk, qkb = qk_pool.next()
                    S.dma('sp', qk, QKB[c], reads=[(DB('QKB'), c)], writes=[qkb])
                    ktm, ktmb = ktm_pool.next()
                    S.dma('sp', ktm, KTM[rows, :], reads=[(DB('KTM'), c)], writes=[ktmb])
                    vv, vvb = v_pool.next()
                    S.dma('sp', vv, V[rows, :], reads=[(DB('V'), c)], writes=[vvb])
                    ga, gab = ga_pool.next()
                    S.dma('sp', ga, GAT[dirn, :, cols], reads=[(DB('GAT'), c)], writes=[gab])
                    X.update(xs=xs, xsb=xsb, btm=btm, btmb=btmb, bct=bct, bctb=bctb, qk=qk, qkb=qkb, ktm=ktm, ktmb=ktmb, vv=vv, vvb=vvb, ga=ga, gab=gab)
                    return X

                def p2(X, pass2):
                    c, dirn = X['c'], X['dirn']
                    rows = slice(c * 128, (c + 1) * 128)
                    Uc16 = U16[:, dirn, :]
                    Ws16 = U16[:, 2 + dirn, :]
                    ahi = ahl_all[:, c, 0, dirn * 16:(dirn + 1) * 16]
                    alo = ahl_all[:, c, 1, dirn * 16:(dirn + 1) * 16]
                    ga, gab = X['ga'], X['gab']
                    if pass2 and X['need']:
                        of, ofb = of_pool.next()
                        S.dma('sp', of, OF[rows, :], reads=[(DB('OF'), c)], writes=[ofb])
                        yf, yfb = yf_pool.next()
                        S.dma('sp', yf, YF[rows, :], reads=[(DB('YF'), c)], writes=[yfb])
                        X.update(of=of, ofb=ofb, yf=yf, yfb=yfb)
                    pa, pb = palloc()
                    lst = []
                    for k3, m in enumerate((Uc16, Ws16, ones16[:])):
                        lst.append((pa[:, k3 * 16:(k3 + 1) * 16], m, ahi, True, False))
                        lst.append((pa[:, k3 * 16:(k3 + 1) * 16], m, alo, False, True))
                    S.op('pe', MM(lst), reads=[U16_b, ones16_b, (ahl_b, c)], writes=pb)
                    sm, smb = sm_pool.next()
                    S.op('act', ACT(sm, pa[:, 0:48], AF.Exp), reads=pb, writes=[smb])
                    rbh, rbhb = rb_pool.next()
                    rbl, rblb = None, None
                    S.op('dve', TT(b3(rbh, 16, 128), Umat[:, dirn, :].unsqueeze(1).to_broadcast([128, 16, 128]),
                                   a_all[:, c, dirn * 16:(dirn + 1) * 16].unsqueeze(2).to_broadcast([128, 16, 128]), ALU.mult),
                         reads=[U_b, (a_b, c)], writes=[rbhb])
                    pa, pb = palloc()
                    S.op('pe', MM([(pa, ga[0:17, :], w2a[0:17, dirn, :], True, True)]), reads=[gab, w2a_b], writes=pb)
                    spt, sptb = sp_pool.next()
                    S.op('act', ACT(spt, pa, AF.Exp, scale=-1.0), reads=pb, writes=[sptb])
                    S.op('act', ACT(spt, spt, AF.Ln, bias=1.0), reads=[sptb], writes=[sptb])
                    sph, sphb = sph_pool.next()
                    S.op('act', ACT(sph[:, 0, :], spt, AF.Copy), reads=[sptb], writes=[sphb])
                    S.op('pool', TT(sph[:, 1, :], spt, sph[:, 0, :], ALU.subtract), reads=[sptb, sphb], writes=[sphb])
                    X.update(sm=sm, smb=smb, rbh=rbh, rbhb=rbhb, rbl=rbl, rblb=rblb, sph=sph, sphb=sphb)

                def p3(X):
                    c, dirn, need_out = X['c'], X['dirn'], X['need']
                    Uc = Umat[:, dirn, :]
                    Uc16 = U16[:, dirn, :]
                    Ws16 = U16[:, 2 + dirn, :]
                    dtc = dt_all[:, c, dirn * 16:(dirn + 1) * 16]
                    xs, xsb, bct, bctb, qk, qkb, ktm, ktmb = X['xs'], X['xsb'], X['bct'], X['bctb'], X['qk'], X['qkb'], X['ktm'], X['ktmb']
                    sm, smb, rbh, rbhb, rbl, rblb, sph, sphb = X['sm'], X['smb'], X['rbh'], X['rbhb'], X['rbl'], X['rblb'], X['sph'], X['sphb']
                    qt, ktp = qk[:, 0:4, :], qk[:, 4:8, :]
                    pa, pb = palloc()
                    lst = []
                    for hh in range(4):
                        lst.append((pa[:, hh * 128:(hh + 1) * 128], sph[:, 0, hh * 128:(hh + 1) * 128], Uc16, True, False))
                        lst.append((pa[:, hh * 128:(hh + 1) * 128], sph[:, 1, hh * 128:(hh + 1) * 128], Uc16, False, True))
                    S.op('pe', MM(lst), reads=[sphb, U16_b], writes=pb)
                    eq, eqb = eq_pool.next()
                    ek, ekb = ek_pool.next()
                    S.op('act', ACT(eq, pa, AF.Exp, scale=-1.0 / 16), reads=pb, writes=[eqb])
                    S.op('act', ACT(ek, pa, AF.Exp, scale=1.0 / 16), reads=pb, writes=[ekb])
                    pa, pb = palloc()
                    S.op('pe', MM([(pa, Ws16, sph[:, 0, :], True, False), (pa, Ws16, sph[:, 1, :], False, True)]), reads=[sphb, U16_b], writes=pb)
                    kes, kesb = kes_pool.next()
                    S.op('act', ACT(kes, pa, AF.Exp, scale=-1.0 / 16), reads=pb, writes=[kesb])
                    pa, pb = palloc()
                    S.op('pe', MM([(pa[:, g * 128:(g + 1) * 128], bct[:, g, :], bct[:, 2 + g, :], True, True) for g in range(2)]),
                         reads=[bctb], writes=pb)
                    cbm, cbmb = cbm_pool.next()
                    S.op('dve', TT(b3(cbm, 2, 128), b3(pa[:, 0:256], 2, 128), Uc.unsqueeze(1).to_broadcast([128, 2, 128]), ALU.mult),
                         reads=pb + [U_b], writes=[cbmb])
                    pD, pDb = palloc(4)
                    lst = []
                    for q in range(4):
                        lst.append((pD[:, q * 512:(q + 1) * 512], Umat[:, 2 + dirn, :], rbh[:, q * 512:(q + 1) * 512], True, True))
                    S.op('pe', MM(lst), reads=[U_b, rbhb], writes=pDb)
                    xdt, xdtb = xdt_pool.next()
                    S.op('pool', TT(b3(xdt, 16, 64), b3(xs, 16, 64), dtc.unsqueeze(2).to_broadcast([128, 16, 64]), ALU.mult),
                         reads=[xsb, (dt_b, c)], writes=[xdtb])
                    xdd, xddb = xdd_pool.next()
                    S.op('pool', TT(b3(xdd, 16, 64), b3(xdt, 16, 64), sm[:, 16:32].unsqueeze(2).to_broadcast([128, 16, 64]), ALU.mult),
                         reads=[xdtb, smb], writes=[xddb])
                    qd, qdb = qd_pool.next()
                    ki, kib = ki_pool.next()
                    ke, keb = ke_pool.next()
                    S.op('dve', TT(qd, qt.rearrange("p a b -> p (a b)"), eq, ALU.mult), reads=[qkb, eqb], writes=[qdb])
                    S.op('dve', TT(ki, ktp.rearrange("p a b -> p (a b)"), ek, ALU.mult), reads=[qkb, ekb], writes=[kib])
                    S.op('pool', TT(ke, ktm, kes, ALU.mult), reads=[ktmb, kesb], writes=[keb])
                    E, Eb = E_pool.next()
                    S.op('act', ACT(E, pD, AF.Exp), reads=pDb, writes=[Eb])
                    att = attb = None
                    if need_out:
                        pa, pb = palloc()
                        S.op('pe', MM([(pa[:, hh * 128:(hh + 1) * 128], ki[:, hh * 128:(hh + 1) * 128], qd[:, hh * 128:(hh + 1) * 128], True, True)
                                       for hh in range(4)]), reads=[kib, qdb], writes=pb)
                        att, attb = att_pool.next()
                        S.op('dve', TT(b3(att, 4, 128), b3(pa, 4, 128), Uc.unsqueeze(1).to_broadcast([128, 4, 128]), ALU.mult),
                             reads=pb + [U_b], writes=[attb])
                    MT, MTb = MT_pool.next()
                    for g in range(2):
                        S.op('dve' if g == 0 else 'pool',
                             TT(b3(MT[:, g * 1024:(g + 1) * 1024], 8, 128), b3(E[:, g * 1024:(g + 1) * 1024], 8, 128),
                                cbm[:, g * 128:(g + 1) * 128].unsqueeze(1).to_broadcast([128, 8, 128]), ALU.mult),
                             reads=[Eb, cbmb], writes=[MTb])
                    X.update(MT=MT, MTb=MTb, xdt=xdt, xdtb=xdtb, xdd=xdd, xddb=xddb, eq=eq, eqb=eqb, qd=qd, qdb=qdb, ke=ke, keb=keb, att=att, attb=attb)

                def finish_a(X, pass2):
                    c, dirn, need_out = X['c'], X['dirn'], X['need']
                    rows = slice(c * 128, (c + 1) * 128)
                    xs, xsb, btm, btmb, bct, bctb, vv, vvb = X['xs'], X['xsb'], X['btm'], X['btmb'], X['bct'], X['bctb'], X['vv'], X['vvb']
                    sm, smb, MT, MTb, xdt, xdtb, xdd, xddb = X['sm'], X['smb'], X['MT'], X['MTb'], X['xdt'], X['xdtb'], X['xdd'], X['xddb']
                    eq, eqb, qd, qdb, ke, keb, att, attb = X['eq'], X['eqb'], X['qd'], X['qdb'], X['ke'], X['keb'], X['att'], X['attb']
                    if need_out:
                        py, pyb = palloc(2)
                        lst = []
                        for h in range(16):
                            o = py[:, h * 64:(h + 1) * 64]
                            lst.append((o, MT[:, h * 128:(h + 1) * 128], xdt[:, h * 64:(h + 1) * 64], True, False))
                            lst.append((o, Dd[:, 0, dirn * 16 + h, :], xs[:, h * 64:(h + 1) * 64], False, False))
                            lst.append((o, Dd[:, 1, dirn * 16 + h, :], xs[:, h * 64:(h + 1) * 64], False, True))
                        S.op('pe', MM(lst), reads=[MTb, xdtb, xsb, Dd_b], writes=pyb)
                        pyo, pyob = palloc(2)
                        S.op('pe', MM([(pyo[:, g * 512:(g + 1) * 512], bct[:, 2 + g, :], hbf[:, g * 512:(g + 1) * 512], True, True)
                                       for g in range(2)]), reads=[bctb, hbf_b], writes=pyob)
                        ys, ysb = ys_pool.next()
                        S.op('dve', TT(b3(ys, 16, 64), b3(pyo, 16, 64), sm[:, 0:16].unsqueeze(2).to_broadcast([128, 16, 64]), ALU.mult),
                             reads=pyob + [smb], writes=[ysb])
                        pyc, pycb = pyc_pool.next()
                        S.op('act', ACT(pyc, py, AF.Copy), reads=pyb, writes=[pycb])
                        S.op('pool', TT(ys, ys, pyc, ALU.add), reads=[ysb, pycb], writes=[ysb])
                        X.update(ys=ys, ysb=ysb)
                    pst_, pstb = palloc(2)
                    S.op('pe', MM([(pst_[:, g * 512:(g + 1) * 512], btm[:, g * 128:(g + 1) * 128], xdd[:, g * 512:(g + 1) * 512], True, True)
                                   for g in range(2)]), reads=[btmb, xddb], writes=pstb)
                    S.op('dve', TT(b3(hst[:], 16, 64), b3(hst[:], 16, 64), sm[:, 32:48].unsqueeze(2).to_broadcast([128, 16, 64]), ALU.mult),
                         reads=[hst_b, smb], writes=[hst_b])
                    S.op('dve', TT(hst[:], hst[:], pst_, ALU.add), reads=[hst_b] + pstb, writes=[hst_b])
                    S.op('act', ACT(hbf[:], hst[:], AF.Copy), reads=[hst_b], writes=[hbf_b])
                    if need_out:
                        po, pob = palloc(2)
                        lst = []
                        for hh in range(4):
                            o = po[:, hh * 256:(hh + 1) * 256]
                            lst.append((o, att[:, hh * 128:(hh + 1) * 128], vv[:, hh * 256:(hh + 1) * 256], True, False))
                            lst.append((o, qd[:, hh * 128:(hh + 1) * 128], Sbf[:, hh * 256:(hh + 1) * 256], False, True))
                        S.op('pe', MM(lst), reads=[attb, vvb, qdb, Sbf_b], writes=pob)
                        osum, osb = os_pool.next()
                        if not pass2:
                            S.op('act', ACT(osum, po, AF.Copy), reads=pob, writes=[osb])
                        else:
                            S.op('dve', TT(osum, po, X['of'], ALU.add), reads=pob + [X['ofb']], writes=[osb])
                            S.op('pool', TT(X['ys'], X['ys'], X['yf'], ALU.add), reads=[X['ysb'], X['yfb']], writes=[X['ysb']])
                        X.update(po=po, pob=pob, osum=osum, osb=osb)
                    pu, pub = palloc(2)
                    S.op('pe', MM([(pu[:, hh * 256:(hh + 1) * 256], ke[:, hh * 128:(hh + 1) * 128], vv[:, hh * 256:(hh + 1) * 256], True, True)
                                   for hh in range(4)]), reads=[keb, vvb], writes=pub)
                    lastcol = 127 if dirn == 0 else 0
                    for hh in range(4):
                        S.op('dve', STT(Sst[:, hh * 256:(hh + 1) * 256], Sst[:, hh * 256:(hh + 1) * 256],
                                        eq[:, hh * 128 + lastcol:hh * 128 + lastcol + 1], pu[:, hh * 256:(hh + 1) * 256], ALU.mult, ALU.add),
                             reads=[(Sst_b, hh), eqb] + pub, writes=[(Sst_b, hh)])
                    S.op('act', ACT(Sbf[:], Sst[:], AF.Copy), reads=[Sst_b], writes=[Sbf_b])
                    if need_out:
                        S.dma('sp', YF[rows, :], X['ys'], reads=[X['ysb']], writes=[(DB('YF'), c)])
                        S.dma('sp', OF[rows, :], X['osum'], reads=[X['osb']], writes=[(DB('OF'), c)])

                for pass2 in (False, True):
                    S.op('pool', MS(hst[:], 0.0), reads=[hst_b], writes=[hst_b])
                    S.op('pool', MS(hbf[:], 0.0), reads=[hbf_b], writes=[hbf_b])
                    S.op('pool', MS(Sst[:], 0.0), reads=[Sst_b], writes=[Sst_b])
                    S.op('pool', MS(Sbf[:], 0.0), reads=[Sbf_b], writes=[Sbf_b])
                    order = list(range(NCH)) if not pass2 else [1, 0] + list(range(NCH - 1, 1, -1))
                    dirn = 1 if pass2 else 0
                    needs = [not (last and c < 2) for c in order]
                    n_ = len(order)
                    Xq = {}
                    for it in range(n_ + 3):
                        if it < n_:
                            Xq[it] = ld(order[it], dirn, needs[it], pass2)
                        if 0 <= it - 1 < n_:
                            p2(Xq[it - 1], pass2)
                        if 0 <= it - 2 < n_:
                            p3(Xq[it - 2])
                        if 0 <= it - 3 < n_:
                            finish_a(Xq[it - 3], pass2)
                            del Xq[it - 3]
                S.flush()

            with ExitStack() as es:
                wo3 = w_out[l].rearrange("(kc p) n -> p kc n", p=128)
                wst_pool = Pool(es, nc, "wstC2", [128, 1024], F32, 3)
                gcol, gcol_b = single(es, nc, "gcol", [128, 16], F32)
                S.dma('sp', gcol[:, 0:8], ssdg_col_in[l], writes=[gcol_b])
                for hh in range(4):
                    S.dma('sp', gcol[:, 8 + 2 * hh:10 + 2 * hh], glag_col_in[l], writes=[gcol_b])
                g1bc, g1bc_b = single(es, nc, "g1bc", [128, 2, D], F32)
                S.dma('sp', g1bc[:], GBC[0].partition_broadcast(128), reads=[DB('GBC', 1)], writes=[g1bc_b])
                nstream = 1 if last else 2
                wouts = []
                for sidx in range(nstream):
                    wo_t, wo_b = single(es, nc, "wout%d" % sidx, [128, 16, D], BF16)
                    wouts.append((wo_t, wo_b))
                for kc in range(16):
                    st_, stb_ = wst_pool.next()
                    S.dma('sp', st_, wo3[:, kc, :], writes=[stb_])
                    for sidx in range(nstream):
                        wo_t, wo_b = wouts[sidx]
                        S.op('dve',
                             STT(wo_t[:, kc, :], st_, gcol[:, kc:kc + 1], g1bc[:, sidx, :], ALU.mult, ALU.mult),
                             reads=[stb_, gcol_b, g1bc_b], writes=[wo_b])
                NB = 4
                ys_pool = Pool(es, nc, "ysE", [128, D], F32, NB)
                os_pool = Pool(es, nc, "osE", [128, D], F32, NB)
                zs_pool = Pool(es, nc, "zsE", [128, D], BF16, NB)
                rs_pool = Pool(es, nc, "rsE", [128, D], BF16, NB)
                hx_pool = Pool(es, nc, "hxE", [128, D], F32, NB + 2)
                st_pool = Pool(es, nc, "stE", [128, 8], F32, NB + 1)
                hd_pool = Pool(es, nc, "hdE", [128, 2 * D], BF16, 2, 2)
                hT_pool = Pool(es, nc, "hTE", [128, 16, 128], BF16, 2, 2)
                junkp = Pool(es, nc, "junkE", [128, D], BF16, 1)

                def b3(ap, a, b_):
                    return ap.rearrange("p (a b) -> p a b", a=a, b=b_)

                def e0(c):
                    X = {'c': c}
                    rows = slice(c * 128, (c + 1) * 128)
                    ys, ysb = ys_pool.next()
                    S.dma('sp', ys, YF[rows, :], reads=[(DB('YF'), c)], writes=[ysb])
                    osum, osb = os_pool.next()
                    S.dma('sp', osum, OF[rows, :], reads=[(DB('OF'), c)], writes=[osb])
                    zs, zsb = zs_pool.next()
                    S.dma('sp', zs, ZS[rows, :], reads=[(DB('ZS'), c)], writes=[zsb])
                    rs, rsb = rs_pool.next()
                    S.dma('sp', rs, RS[rows, :], reads=[(DB('RS'), c)], writes=[rsb])
                    hx, hxb = hx_pool.next()
                    S.dma('sp', hx, h_src(c), reads=[(DB('H2'), c)] if l > 0 else [], writes=[hxb])
                    X.update(ys=ys, ysb=ysb, osum=osum, osb=osb, zs=zs, zsb=zsb, rs=rs, rsb=rsb, hx=hx, hxb=hxb)
                    return X

                def e1(X):
                    ys, ysb, osum, osb, zs, zsb = X['ys'], X['ysb'], X['osum'], X['osb'], X['zs'], X['zsb']
                    st, stb = st_pool.next()
                    S.op('pool', MS(st, 0.0), writes=[stb])
                    junk, junkb = junkp.next()
                    S.op('pool', TT(ys, ys, zs, ALU.mult), reads=[ysb, zsb], writes=[ysb])
                    for g in range(2):
                        S.op('act', ACT(junk[:, g * 512:(g + 1) * 512], ys[:, g * 512:(g + 1) * 512], AF.Square, accum_out=st[:, g:g + 1]),
                             reads=[ysb, stb], writes=[stb])
                    for hh in range(4):
                        S.op('act', ACT(junk[:, hh * 256:(hh + 1) * 256], osum[:, hh * 256:(hh + 1) * 256], AF.Square, accum_out=st[:, 2 + hh:3 + hh]),
                             reads=[osb, stb], writes=[stb])
                    X.update(st=st, stb=stb)

                def e2(X):
                    ys, ysb, osum, osb, rs, rsb, st, stb = X['ys'], X['ysb'], X['osum'], X['osb'], X['rs'], X['rsb'], X['st'], X['stb']
                    S.op('dve', TS(st[:, 0:2], st[:, 0:2], 1.0 / 512, EPS, ALU.mult, ALU.add), reads=[stb], writes=[stb])
                    S.op('dve', TS(st[:, 2:6], st[:, 2:6], 1.0 / 256, EPS, ALU.mult, ALU.add), reads=[stb], writes=[stb])
                    S.op('act', ACT(st[:, 0:6], st[:, 0:6], AF.Sqrt), reads=[stb], writes=[stb])
                    S.op('dve', RECIP(st[:, 0:6], st[:, 0:6]), reads=[stb], writes=[stb])
                    hd, hdb = hd_pool.next()
                    for g in range(2):
                        S.op('act', ACT(hd[:, g * 512:(g + 1) * 512], ys[:, g * 512:(g + 1) * 512], AF.Copy, scale=st[:, g:g + 1]),
                             reads=[ysb, stb], writes=[(hdb, 0)])
                    for hh in range(4):
                        S.op('dve',
                             STT(hd[:, D + hh * 256:D + (hh + 1) * 256], osum[:, hh * 256:(hh + 1) * 256], st[:, 2 + hh:3 + hh],
                                 rs[:, hh * 256:(hh + 1) * 256], ALU.mult, ALU.mult),
                             reads=[osb, stb, rsb], writes=[(hdb, 1)])
                    X.update(hd=hd, hdb=hdb)

                def e3(X):
                    c = X['c']
                    rows = slice(c * 128, (c + 1) * 128)
                    sidx = 1 if c < 2 else 0
                    wo_t, wo_b = wouts[sidx]
                    hd, hdb, hx, hxb = X['hd'], X['hdb'], X['hx'], X['hxb']
                    hT, hTb = hT_pool.next()
                    for half in range(2):
                        pa, pb = palloc()
                        pT = pa.bitcast(BF16)
                        S.op('pe', TR([(pT[:, k8 * 128:(k8 + 1) * 128], hd[:, (half * 8 + k8) * 128:(half * 8 + k8 + 1) * 128], ident[:])
                                       for k8 in range(8)]), reads=[(hdb, half), ident_b], writes=pb)
                        dst = hT[:, half * 8:(half + 1) * 8, :].rearrange("p a b -> p (a b)")
                        if half == 0:
                            S.op('act', ACT(dst, pT, AF.Copy), reads=pb, writes=[(hTb, half)])
                        else:
                            S.op('dve', CP(dst, pT), reads=pb, writes=[(hTb, half)])
                    for nh in range(2):
                        pa, pb = palloc()
                        S.op('pe', MM([(pa, hT[:, kc, :], wo_t[:, kc, nh * 512:(nh + 1) * 512], kc == 0, kc == 15) for kc in range(16)]),
                             reads=[hTb, wo_b], writes=pb)
                        S.op('dve', TT(hx[:, nh * 512:(nh + 1) * 512], pa, hx[:, nh * 512:(nh + 1) * 512], ALU.add), reads=pb + [hxb], writes=[hxb])
                    S.dma('sp', H1[rows, :], hx, reads=[hxb], writes=[(DB('H1'), c)])

                chs = [c for c in range(NCH) if not (last and c < 2)]
                Xs = {}
                for k in range(len(chs) + 3):
                    if k < len(chs):
                        Xs[k] = e0(chs[k])
                    if 0 <= k - 1 < len(chs):
                        e1(Xs[k - 1])
                    if 0 <= k - 2 < len(chs):
                        e2(Xs[k - 2])
                    if 0 <= k - 3 < len(chs):
                        e3(Xs[k - 3])
                        del Xs[k - 3]
                S.flush()
            if stop_after == 'C':
                les.close()
                break

            les.close()
            with ExitStack() as es:
                w1, w1_b = single(es, nc, "w1", [128, KD, 4 * D], BF16, 4)
                w2, w2_b = single(es, nc, "w2", [128, 32, D], BF16)
                w13 = w_ff1[l].rearrange("(kc p) n -> p kc n", p=128)
                w23 = w_ff2[l].rearrange("(kc p) n -> p kc n", p=128)
                wst_pool = Pool(es, nc, "wstD", [128, 512], F32, 4)
                for blk in range(4):
                    for kc in range(KD):
                        load_cast(wst_pool, w1[:, kc, blk * 1024:(blk + 1) * 1024], w13[:, kc, blk * 1024:(blk + 1) * 1024], (w1_b, blk), kc + blk)
                for kc in range(32):
                    load_cast(wst_pool, w2[:, kc, :], w23[:, kc, :], w2_b, kc)
                g2bc, g2bc_b = single(es, nc, "g2bc", [128, 2, D], F32)
                S.dma('sp', g2bc[:], GBC[1].partition_broadcast(128), reads=[DB('GBC', 1)], writes=[g2bc_b])
                if last:
                    fng, fng_b = single(es, nc, "fng", [128, D], F32)
                    S.dma('sp', fng[:], fng_in.partition_broadcast(128), writes=[fng_b])
                TD = 256
                hx_pool = Pool(es, nc, "hxD", [128, D], F32, 4)
                ssp = Pool(es, nc, "ssD", [128, 4], F32, 4)
                xnp = Pool(es, nc, "xnD", [128, D], BF16, 2)
                uT_pool = Pool(es, nc, "uTD", [128, KD, TD], BF16, 2, 2)
                hid_pool = Pool(es, nc, "hidD", [128, 32, TD], BF16, 1, 32)
                rl_pool = Pool(es, nc, "rlD", [128, TD], F32, 3)
                g_pool = Pool(es, nc, "gD", [128, 512], F32, 2)
                st_pool = Pool(es, nc, "stD", [128, 4], F32, 2)
                tilesD = [(t0, TD) for t0 in range(0, T, TD)]
                tilesD = [(t0, n_) for (t0, n_) in tilesD if not (last and t0 < CTX)]

                def preD(tok0, ntok):
                    hxs_, xns_ = [], []
                    for ci in range(ntok // 128):
                        c = tok0 // 128 + ci
                        hx, hxb = hx_pool.next()
                        S.dma('sp', hx, H1[c * 128:(c + 1) * 128, :], reads=[(DB('H1'), c)], writes=[hxb])
                        xns_.append(norm_pre((ssp, xnp), hx, hxb))
                        hxs_.append((hx, hxb))
                    return hxs_, xns_

                def peD(xns_, tok0):
                    uT_, uTb_ = uT_pool.next()
                    s_ = 1 if tok0 < CTX else 0
                    for ci, (xn, xnb) in enumerate(xns_):
                        norm_pe(xn, xnb, ci, uT_, uTb_, A2c, A2_b, 24, s_)
                    return uT_, uTb_

                h0_, x0_ = preD(*tilesD[0])
                dq = {0: (h0_, peD(x0_, tilesD[0][0]))}
                for ti, (tok0, ntok) in enumerate(tilesD):
                    nchk = ntok // 128
                    s = 1 if tok0 < CTX else 0
                    hxs, (uT, uTb) = dq.pop(ti)
                    nxt = preD(*tilesD[ti + 1]) if ti + 1 < len(tilesD) else None
                    hid, hidb = hid_pool.next()
                    for hc in range(32):
                        pa, pb = palloc()
                        S.op('pe', MM([(pa[:, 0:ntok], w1[:, kc, hc * 128:(hc + 1) * 128], uT[:, kc, 0:ntok], kc == 0, kc == KD - 1) for kc in range(KD)]),
                             reads=[uTb, (w1_b, hc // 8)], writes=pb)
                        rl, rlb = rl_pool.next()
                        S.op('act', ACT(rl[:, 0:ntok], pa[:, 0:ntok], AF.Relu), reads=pb, writes=[rlb])
                        S.op('pool' if hc % 2 == 0 else 'dve', TT(hid[:, hc, 0:ntok], rl[:, 0:ntok], rl[:, 0:ntok], ALU.mult), reads=[rlb], writes=[(hidb, hc)])
                    if nxt is not None:
                        dq[ti + 1] = (nxt[0], peD(nxt[1], tilesD[ti + 1][0]))
                    for ci in range(nchk):
                        c = tok0 // 128 + ci
                        hx, hxb = hxs[ci]
                        for nh in range(2):
                            pa, pb = palloc()
                            S.op('pe', MM([(pa, hid[:, hc, ci * 128:(ci + 1) * 128], w2[:, hc, nh * 512:(nh + 1) * 512], hc == 0, hc == 31) for hc in range(32)]),
                                 reads=[hidb, w2_b], writes=pb)
                            gt, gtb = g_pool.next()
                            S.op('dve', TT(gt, pa, g2bc[:, s, nh * 512:(nh + 1) * 512], ALU.mult), reads=pb + [g2bc_b], writes=[gtb])
                            S.op('pool', TT(hx[:, nh * 512:(nh + 1) * 512], hx[:, nh * 512:(nh + 1) * 512], gt, ALU.add), reads=[gtb, hxb], writes=[hxb])
                        if not last:
                            S.dma('sp', H2[c * 128:(c + 1) * 128, :], hx, reads=[hxb], writes=[(DB('H2'), c)])
                        else:
                            st, stb = st_pool.next()
                            xn, xnb = xnp.next()
                            S.op('dve', MS(st[:, 0:1], 0.0), writes=[stb])
                            S.op('act', ACT(xn, hx, AF.Square, accum_out=st[:, 0:1]), reads=[hxb, stb], writes=[xnb, stb])
                            S.op('act', ACT(st[:, 1:2], st[:, 0:1], AF.Sqrt, scale=1.0 / D, bias=eps_t[:, 0:1]), reads=[stb, eps_b], writes=[stb])
                            S.op('dve', RECIP(st[:, 2:3], st[:, 1:2]), reads=[stb], writes=[stb])
                            S.op('dve', STT(hx, hx, st[:, 2:3], fng[:], ALU.mult, ALU.mult), reads=[hxb, stb, fng_b], writes=[hxb])
                            S.dma('sp', out[(c - 2) * 128:(c - 1) * 128, :], hx, reads=[hxb], writes=[DB('OUT')])
                S.flush()
    print("ops recorded:", S.nops)
    return nc


def make_in_maps(inputs, L, ncores):
    f = lambda a: np.ascontiguousarray(np.asarray(a, dtype=np.float32))
    col = lambda v, n: f(np.asarray(v).reshape(n, 128).T)
    shared = {
        "w_ada": f(inputs["w_ada"]),
        "bada_row": f(np.asarray(inputs["b_ada"]).reshape(DEPTH, 1, 6 * D)),
        "bada_col": f(np.stack([col(inputs["b_ada"][l], 48) for l in range(DEPTH)])),
        "n1g_col": f(np.stack([col(inputs["norm1_g"][l], KD) for l in range(DEPTH)])),
        "n2g_col": f(np.stack([col(inputs["norm2_g"][l], KD) for l in range(DEPTH)])),
        "w_in": f(inputs["w_in"]),
        "convw_col": f(np.asarray(inputs["conv_w"]).reshape(DEPTH, 9, 12, 128).transpose(0, 3, 2, 1)),
        "convb_col": f(np.asarray(inputs["conv_b"]).reshape(DEPTH, 12, 128).transpose(0, 2, 1)),
        "convb_row": f(np.asarray(inputs["conv_b"]).reshape(DEPTH, 1, 1536)),
        "dt_bias": f(np.asarray(inputs["dt_bias"]).reshape(DEPTH, 32)),
        "a_log": f(np.asarray(inputs["a_log"]).reshape(DEPTH, 32)),
        "d_skip": f(np.asarray(inputs["d_skip"]).reshape(DEPTH, 32)),
        "ssdg_col": f(np.stack([col(inputs["ssd_norm_g"][l], 8) for l in range(DEPTH)])),
        "glag_col": f(np.stack([col(inputs["gla_norm_g"][l], 2) for l in range(DEPTH)])),
        "w2aug": f(np.concatenate([np.asarray(inputs["gla_w2"]), np.asarray(inputs["gla_b2"])[:, :, None, :]], axis=2)),
        "w_out": f(inputs["w_out"]),
        "w_ff1": f(inputs["w_ff1"]),
        "w_ff2": f(inputs["w_ff2"]),
        "final_norm_g": f(inputs["final_norm_g"]),
    }
    maps = []
    cc = col(inputs["c_ctx"], KD)
    for b in range(ncores):
        m = dict(shared)
        m["x"] = f(inputs["x"][b])
        m["ctx"] = f(inputs["ctx"][b])
        m["ccol"] = f(np.stack([col(inputs["c"][b], KD), cc], axis=-1))
        maps.append(m)
    return maps


_NC_CACHE = {}


def kernel(**inputs):
    x = np.asarray(inputs["x"])
    B, L, _ = x.shape
    if L not in _NC_CACHE:
        _NC_CACHE[L] = build(L)
    nc = _NC_CACHE[L]
    maps = make_in_maps(inputs, L, B)
    res = run_bass_kernel_spmd(nc, maps, core_ids=list(range(B)))
    return np.stack([np.asarray(r["out"], dtype=np.float32) for r in res.results], axis=0)
```

```python
import numpy as np
from collections import defaultdict
from contextlib import ExitStack
import concourse.bass as bass
import concourse.mybir as mybir
from concourse.bass_utils import run_bass_kernel_spmd

F32 = mybir.dt.float32
BF16 = mybir.dt.bfloat16
AF = mybir.ActivationFunctionType
ALU = mybir.AluOpType

D = 1024
KD = 8
CTX = 256
DEPTH = 2
EPS = 1e-6
ENG = ('sp', 'act', 'dve', 'pool', 'pe')
NQ = 8


class Buf:
    def __init__(self, nparts=1):
        self.n = nparts
        self.lw = [None] * nparts
        self.rd = [dict() for _ in range(nparts)]


def _expand(lst):
    for it in lst:
        if it is None:
            continue
        if isinstance(it, tuple):
            yield it
        elif isinstance(it, list):
            for x in _expand(it):
                yield x
        else:
            for i in range(it.n):
                yield (it, i)


class Sched:
    def __init__(self, nc):
        self.nc = nc
        self.h = {}
        for e in ('pe', 'act', 'dve', 'pool'):
            self.h[e] = nc.alloc_semaphore('s_' + e)
        for q in ('sp', 'pool', 'act'):
            for i in range(NQ):
                self.h[(q, i)] = nc.alloc_semaphore('d_%s%d' % (q, i))
        self.cnt = defaultdict(int)
        self.prog = {e: [] for e in ENG}
        self.waited = {e: {} for e in ENG}
        self.rr = defaultdict(int)
        self.nops = 0

    def _need(self, eng, deps):
        w = self.waited[eng]
        for k, v in deps.items():
            if w.get(k, 0) < v:
                self.prog[eng].append(('w', k, v))
                w[k] = v

    def _deps(self, reads, writes):
        deps = {}

        def add(ev):
            if ev is not None:
                k, v = ev
                if deps.get(k, 0) < v:
                    deps[k] = v
        for b, i in _expand(reads):
            add(b.lw[i])
        for b, i in _expand(writes):
            add(b.lw[i])
            for k, v in b.rd[i].items():
                add((k, v))
        return deps

    def _commit(self, ev, reads, writes):
        k, v = ev
        for b, i in _expand(reads):
            if b.rd[i].get(k, 0) < v:
                b.rd[i][k] = v
        for b, i in _expand(writes):
            b.lw[i] = ev
            b.rd[i] = {}

    def op(self, eng, fn, reads=(), writes=()):
        deps = self._deps(reads, writes)
        self._need(eng, deps)
        self.cnt[eng] += 1
        ev = (eng, self.cnt[eng])
        self.prog[eng].append(('o', fn, eng, 1))
        self._commit(ev, reads, writes)
        self.nops += 1

    def dma(self, q, out, in_, reads=(), writes=()):
        i = self.rr[q]
        self.rr[q] = (i + 1) % NQ
        key = (q, i)
        deps = self._deps(reads, writes)
        if self.cnt[key] > 0:
            deps[key] = max(deps.get(key, 0), self.cnt[key])
        self._need(q, deps)
        self.cnt[key] += 16
        ev = (key, self.cnt[key])
        self.prog[q].append(('o', (lambda e, o=out, s=in_: e.dma_start(out=o, in_=s)), key, 16))
        self._commit(ev, reads, writes)
        self.nops += 1

    def flush(self):
        final = {k: v for k, v in self.cnt.items() if v > 0}
        for e in ENG:
            self._need(e, final)
        progs = self.prog
        self.prog = {e: [] for e in ENG}
        hmap = self.h
        with self.nc.Block() as block:
            def mk(prog):
                def body(eng):
                    for item in prog:
                        if item[0] == 'w':
                            eng.wait_ge(hmap[item[1]], item[2])
                        else:
                            ins = item[1](eng)
                            ins.then_inc(hmap[item[2]], item[3])
                return body
            block.sync(mk(progs['sp']))
            block.scalar(mk(progs['act']))
            block.vector(mk(progs['dve']))
            block.gpsimd(mk(progs['pool']))
            block.tensor(mk(progs['pe']))


def MM(lst):
    def f(e):
        ins = None
        for (o, l, r, s, t) in lst:
            ins = e.matmul(o, l, r, start=s, stop=t)
        return ins
    return f


def TR(lst):
    def f(e):
        ins = None
        for (o, i, idn) in lst:
            ins = e.transpose(o, i, idn)
        return ins
    return f


def ACT(out, in_, func, **kw):
    return lambda e: e.activation(out=out, in_=in_, func=func, **kw)


def TT(out, in0, in1, op):
    return lambda e: e.tensor_tensor(out=out, in0=in0, in1=in1, op=op)


def TS(out, in0, s1, s2, op0, op1=None):
    if op1 is None:
        return lambda e: e.tensor_scalar(out=out, in0=in0, scalar1=s1, scalar2=None, op0=op0)
    return lambda e: e.tensor_scalar(out=out, in0=in0, scalar1=s1, scalar2=s2, op0=op0, op1=op1)


def STT(out, in0, scalar, in1, op0, op1):
    return lambda e: e.scalar_tensor_tensor(out=out, in0=in0, scalar=scalar, in1=in1, op0=op0, op1=op1)


def CP(out, in_):
    return lambda e: e.tensor_copy(out=out, in_=in_)


def MS(out, val):
    return lambda e: e.memset(out, val)


def RECIP(out, in_):
    return lambda e: e.reciprocal(out=out, in_=in_)


_UID = [0]


class Pool:
    def __init__(self, es, nc, name, shape, dtype, n, nparts=1):
        _UID[0] += 1
        self.t = es.enter_context(nc.sbuf_tensor("%s_%d" % (name, _UID[0]), [shape[0], n] + list(shape[1:]), dtype))
        self.n = n
        self.bufs = [Buf(nparts) for _ in range(n)]
        self.i = 0

    def next(self):
        i = self.i
        self.i = (i + 1) % self.n
        return self.t[:, i], self.bufs[i]


def single(es, nc, name, shape, dtype, nparts=1):
    _UID[0] += 1
    t = es.enter_context(nc.sbuf_tensor("%s_%d" % (name, _UID[0]), list(shape), dtype))
    return t, Buf(nparts)


def build(L, debug=False, nlayers=DEPTH, stop_after=None):
    T = CTX + L
    NCH = T // 128
    TP = T + 384
    nc = bass.Bass("TRN2", target_bir_lowering=False)
    S = Sched(nc)
    okind = "ExternalOutput" if debug else "Internal"

    def din(name, shape, dt=F32):
        return nc.dram_tensor(name, list(shape), dt, kind="ExternalInput").ap()

    def dscr(name, shape, dt):
        return nc.dram_tensor(name, list(shape), dt, kind=okind).ap()

    x_in = din("x", [L, D])
    ctx_in = din("ctx", [CTX, D])
    ccol_in = din("ccol", [128, KD, 2])
    w_ada = din("w_ada", [DEPTH, D, 6 * D])
    bada_row = din("bada_row", [DEPTH, 1, 6 * D])
    bada_col = din("bada_col", [DEPTH, 128, 48])
    n1g_col = din("n1g_col", [DEPTH, 128, KD])
    n2g_col = din("n2g_col", [DEPTH, 128, KD])
    w_in = din("w_in", [DEPTH, D, 5696])
    convw_col = din("convw_col", [DEPTH, 128, 12, 9])
    convb_col = din("convb_col", [DEPTH, 128, 12])
    convb_row = din("convb_row", [DEPTH, 1, 1536])
    dtb_in = din("dt_bias", [DEPTH, 32])
    alog_in = din("a_log", [DEPTH, 32])
    dskip_in = din("d_skip", [DEPTH, 32])
    ssdg_col_in = din("ssdg_col", [DEPTH, 128, 8])
    glag_col_in = din("glag_col", [DEPTH, 128, 2])
    w2aug_in = din("w2aug", [DEPTH, 2, 17, 512])
    w_out = din("w_out", [DEPTH, 2 * D, D])
    w_ff1 = din("w_ff1", [DEPTH, D, 4 * D])
    w_ff2 = din("w_ff2", [DEPTH, 4 * D, D])
    fng_in = din("final_norm_g", [D])
    out = nc.dram_tensor("out", [L, D], F32, kind="ExternalOutput").ap()

    H1 = dscr("H1", [T, D], F32)
    H2 = dscr("H2", [T, D], F32)
    ZS = dscr("ZS", [T, D], BF16)
    RS = dscr("RS", [T, D], BF16)
    V = dscr("V", [T, D], BF16)
    KTM = dscr("KTM", [T, 512], BF16)
    XBCT = dscr("XBCT", [1536, TP], BF16)
    QKB = dscr("QKB", [NCH, 128, 8, 128], BF16)
    GAT = dscr("GAT", [2, 17, T], F32)
    XS = dscr("XS", [T, D], BF16)
    BTM = dscr("BTM", [T, 256], BF16)
    BCB = dscr("BCB", [NCH, 128, 4, 128], BF16)
    GBC = dscr("GBC", [2, 2, D], F32)
    YF = dscr("YF", [T, D], F32)
    OF = dscr("OF", [T, D], F32)
    dram_bufs = {}

    def DB(name, n=NCH):
        if name not in dram_bufs:
            dram_bufs[name] = Buf(n)
        return dram_bufs[name]

    def xcol(t):
        return t + 128 if t < CTX else t + 256

    tiles = [(0, CTX)] + [(CTX + i * 512, 512) for i in range(L // 512)]

    with ExitStack() as ges:
        ps = ges.enter_context(nc.psum_tensor("ps", [128, 8, 512], F32))
        pbufs = [Buf() for _ in range(8)]
        pst = {'i': 0}

        def palloc(n=1):
            i = pst['i']
            if n > 1:
                i = ((i + n - 1) // n) * n
            if i + n > 8:
                i = 0
            pst['i'] = (i + n) % 8
            ap = ps[:, i:i + n, :].rearrange("p b f -> p (b f)") if n > 1 else ps[:, i, :]
            return ap, pbufs[i:i + n]

        def load_cast(stage_pool, dst, src, dstb, i, q='sp'):
            n = dst.shape[-1]
            step = stage_pool.t.shape[-1]
            for o in range(0, n, step):
                m = min(step, n - o)
                st_, stb_ = stage_pool.next()
                S.dma(q, st_[:, 0:m], src[:, o:o + m], writes=[stb_])
                eng = ('dve', 'act', 'pool')[(i + o // step) % 3]
                if eng == 'act':
                    S.op('act', ACT(dst[:, o:o + m], st_[:, 0:m], AF.Copy), reads=[stb_], writes=[dstb])
                else:
                    S.op(eng, CP(dst[:, o:o + m], st_[:, 0:m]), reads=[stb_], writes=[dstb])

        ident, ident_b = single(ges, nc, "ident", [128, 128], BF16)
        Umat, U_b = single(ges, nc, "Umat", [128, 4, 128], F32)
        ones32, ones32_b = single(ges, nc, "ones32", [128, 128], F32)
        ones16, ones16_b = single(ges, nc, "ones16", [128, 128], BF16)
        S.op('pool', MS(ident[:], 0.0), writes=[ident_b])
        S.op('pool', lambda e: e.affine_select(out=ident[:], in_=ident[:], pattern=[[-1, 128]], compare_op=ALU.not_equal,
                                                fill=1.0, base=0, channel_multiplier=1), reads=[ident_b], writes=[ident_b])
        S.op('pool', MS(Umat[:], 1.0), writes=[U_b])
        for mi, (pat, cm, cop) in enumerate([(1, -1, ALU.is_ge), (-1, 1, ALU.is_ge), (-1, 1, ALU.is_gt), (1, -1, ALU.is_gt)]):
            S.op('pool', (lambda e, mi=mi, pat=pat, cm=cm, cop=cop: e.affine_select(
                out=Umat[:, mi, :], in_=Umat[:, mi, :], pattern=[[pat, 128]], compare_op=cop, fill=0.0, base=0,
                channel_multiplier=cm)), reads=[U_b], writes=[U_b])
        U16, U16_b = single(ges, nc, "U16", [128, 4, 128], BF16)
        S.op('pool', CP(U16[:], Umat[:]), reads=[U_b], writes=[U16_b])
        S.op('pool', MS(ones32[:], 1.0), writes=[ones32_b])
        S.op('pool', MS(ones16[:], 1.0), writes=[ones16_b])
        xb3 = XBCT.rearrange("(cc p) t -> p cc t", p=128)

        ssb, ss_b = single(ges, nc, "s_sb", [128, KD, 2], F32)
        modc, modc_b = single(ges, nc, "modc", [128, 48, 2], F32)
        A1c, A1_b = single(ges, nc, "A1c", [128, KD, 2], F32)
        A2c, A2_b = single(ges, nc, "A2c", [128, KD, 2], F32)
        eps_t, eps_b = single(ges, nc, "eps_t", [128, 1], F32)
        S.op('pool', MS(eps_t[:], EPS), writes=[eps_b])
        S.dma('sp', ssb[:], ccol_in, writes=[ss_b])
        S.op('act', ACT(ssb[:], ssb[:], AF.Silu), reads=[ss_b], writes=[ss_b])
        with ExitStack() as pes:
            zero16, zero16_b = single(pes, nc, "zero16", [128, 12, 128], BF16)
            S.op('pool', MS(zero16[:], 0.0), writes=[zero16_b])
            for pad0 in (0, 128 + CTX, TP - 128):
                S.dma('sp', xb3[:, :, pad0:pad0 + 128], zero16[:], reads=[zero16_b], writes=[DB('XBCT', 1)])
            S.flush()

        for l in range(nlayers):
            last = (l == DEPTH - 1)

            def h_src(c):
                if l == 0:
                    if c < 2:
                        return ctx_in[c * 128:(c + 1) * 128, :]
                    return x_in[(c - 2) * 128:(c - 1) * 128, :]
                return H2[c * 128:(c + 1) * 128, :]

            les = ExitStack()
            dt_all, dt_b = single(les, nc, "dt_all", [128, NCH, 32], F32, NCH)
            a_all, a_b = single(les, nc, "a_all", [128, NCH, 32], F32, NCH)
            ahl_all, ahl_b = single(les, nc, "ahl_all", [128, NCH, 2, 32], BF16, NCH)
            wes = ExitStack()
            wtm, wtm_b = single(wes, nc, "wtm", [128, KD, 3616], BF16)
            wfm, wfm_b = single(wes, nc, "wfm", [128, KD, 2592], BF16)
            w_in3 = w_in[l].rearrange("(kc p) n -> p kc n", p=128)
            wstA_pool = Pool(wes, nc, "wstA", [128, 512], F32, 4)
            wi = 0
            for kc in range(KD):
                for (d0, s0, n) in [(0, 0, 1024), (1024, 3104, 512), (1536, 3616, 1024), (2560, 4640, 1024), (3584, 2560, 32)]:
                    load_cast(wstA_pool, wtm[:, kc, d0:d0 + n], w_in3[:, kc, s0:s0 + n], wtm_b, wi, q='act')
                    wi += 1
                for (d0, s0, n) in [(0, 1024, 1536), (1536, 2592, 1024), (2560, 5664, 32)]:
                    load_cast(wstA_pool, wfm[:, kc, d0:d0 + n], w_in3[:, kc, s0:s0 + n], wfm_b, wi, q='act')
                    wi += 1

            with ExitStack() as es:
                wa_pool = Pool(es, nc, "wa", [128, KD, 512], F32, 3)
                brow, brow_b = single(es, nc, "brow", [1, 2 * D], F32)
                bcol, bcol_b = single(es, nc, "bcol", [128, 48], F32)
                n1c, n1c_b = single(es, nc, "n1c", [128, KD], F32)
                n2c, n2c_b = single(es, nc, "n2c", [128, KD], F32)
                srep, srep_b = single(es, nc, "srep", [128, KD, 2, 128], F32)
                S.op('dve', CP(srep[:], ssb[:].unsqueeze(3).to_broadcast([128, KD, 2, 128])), reads=[ss_b], writes=[srep_b])
                grow_pool = Pool(es, nc, "grow", [128, 512], F32, 2)
                S.dma('sp', brow[0:1, 0:D], bada_row[l][0:1, 2 * D:3 * D], writes=[brow_b])
                S.dma('sp', brow[0:1, D:2 * D], bada_row[l][0:1, 5 * D:6 * D], writes=[brow_b])
                S.dma('sp', bcol[:], bada_col[l], writes=[bcol_b])
                S.dma('sp', n1c[:], n1g_col[l], writes=[n1c_b])
                S.dma('sp', n2c[:], n2g_col[l], writes=[n2c_b])
                for j in (0, 1, 2, 3, 6, 7, 8, 9, 4, 5, 10, 11):
                    wa, wab = wa_pool.next()
                    S.dma('sp', wa, w_ada[l][:, j * 512:(j + 1) * 512].rearrange("(kc p) n -> p kc n", p=128), writes=[wab])
                    idx, half = j // 2, j % 2
                    if idx in (2, 5):
                        for s in range(2):
                            pa, pb = palloc()
                            lst = [(pa, srep[:, kc, s, :], wa[:, kc, :], kc == 0, False) for kc in range(KD)]
                            gi_ = 0 if idx == 2 else 1
                            lst.append((pa, ones32[0:1, :], brow[0:1, gi_ * D + half * 512:gi_ * D + (half + 1) * 512], False, True))
                            S.op('pe', MM(lst), reads=[wab, srep_b, brow_b, ones32_b], writes=pb)
                            gi = 0 if idx == 2 else 1
                            gr, grb = grow_pool.next()
                            S.op('dve', CP(gr, pa), reads=pb, writes=[grb])
                            S.dma('sp', GBC[gi, s:s + 1, half * 512:(half + 1) * 512], gr[0:1, :], reads=[grb], writes=[DB('GBC', 1)])
                    else:
                        pa, pb = palloc()
                        lst = []
                        for n4 in range(4):
                            for kc in range(KD):
                                lst.append((pa[:, n4 * 2:n4 * 2 + 2], wa[:, kc, n4 * 128:(n4 + 1) * 128], ssb[:, kc, :], kc == 0, kc == KD - 1))
                        S.op('pe', MM(lst), reads=[wab, ss_b], writes=pb)
                        S.op('dve', TT(modc[:, j * 4:(j + 1) * 4, :], pa[:, 0:8].rearrange("p (a b) -> p a b", b=2),
                                       bcol[:, j * 4:(j + 1) * 4].unsqueeze(2).to_broadcast([128, 4, 2]), ALU.add),
                             reads=pb + [bcol_b], writes=[modc_b])
                S.op('dve', STT(A1c[:], modc[:, 8:16, :], 1.0, n1c[:].unsqueeze(2).to_broadcast([128, KD, 2]), ALU.add, ALU.mult),
                     reads=[modc_b, n1c_b], writes=[A1_b])
                S.op('dve', STT(A2c[:], modc[:, 32:40, :], 1.0, n2c[:].unsqueeze(2).to_broadcast([128, KD, 2]), ALU.add, ALU.mult),
                     reads=[modc_b, n2c_b], writes=[A2_b])
                S.flush()

            def norm_T(pools, ci, hx, hxb, uT, uTb, Ac, Ab, shoff, s):
                xn, xnb = norm_pre(pools, hx, hxb)
                norm_pe(xn, xnb, ci, uT, uTb, Ac, Ab, shoff, s)

            def norm_pre(pools, hx, hxb):
                ssp, xnp = pools
                ss, ssb_ = ssp.next()
                xn, xnb = xnp.next()
                S.op('dve', MS(ss[:, 0:1], 0.0), writes=[ssb_])
                S.op('act', ACT(xn, hx, AF.Square, accum_out=ss[:, 0:1]), reads=[hxb, ssb_], writes=[xnb, ssb_])
                S.op('act', ACT(ss[:, 1:2], ss[:, 0:1], AF.Sqrt, scale=1.0 / D, bias=eps_t[:, 0:1]), reads=[ssb_, eps_b], writes=[ssb_])
                S.op('dve', RECIP(ss[:, 2:3], ss[:, 1:2]), reads=[ssb_], writes=[ssb_])
                S.op('dve', TS(xn, hx, ss[:, 2:3], None, ALU.mult), reads=[hxb, ssb_], writes=[xnb])
                return xn, xnb

            def norm_pe(xn, xnb, ci, uT, uTb, Ac, Ab, shoff, s):
                pa, pb = palloc()
                pT = pa.bitcast(BF16)
                S.op('pe', TR([(pT[:, kc * 128:(kc + 1) * 128], xn[:, kc * 128:(kc + 1) * 128], ident[:]) for kc in range(KD)]),
                     reads=[xnb, ident_b], writes=pb)
                for kc in range(KD):
                    o = uT[:, kc, ci * 128:(ci + 1) * 128]
                    i_ = pT[:, kc * 128:(kc + 1) * 128]
                    sc_ = Ac[:, kc, s:s + 1]
                    bi_ = modc[:, shoff + kc, s:s + 1]
                    if kc % 2 == 0:
                        S.op('act', ACT(o, i_, AF.Identity, scale=sc_, bias=bi_), reads=pb + [Ab, modc_b], writes=[(uTb, ci)])
                    else:
                        S.op('dve', TS(o, i_, sc_, bi_, ALU.mult, ALU.add), reads=pb + [Ab, modc_b], writes=[(uTb, ci)])

            with ExitStack() as es:
                dtb_bc, dtb_b = single(es, nc, "dtb_bc", [128, 32], F32)
                A_bc, A_b = single(es, nc, "A_bc", [128, 32], F32)
                S.dma('sp', dtb_bc[:], dtb_in[l].partition_broadcast(128), writes=[dtb_b])
                S.dma('sp', A_bc[:], alog_in[l].partition_broadcast(128), writes=[A_b])
                S.op('act', ACT(A_bc[:], A_bc[:], AF.Exp), reads=[A_b], writes=[A_b])
                S.op('dve', TS(A_bc[:], A_bc[:], -1.0, None, ALU.mult), reads=[A_b], writes=[A_b])
                hx_pool = Pool(es, nc, "hxA", [128, D], F32, 3)
                ssp = Pool(es, nc, "ssA", [128, 4], F32, 4)
                xnp = Pool(es, nc, "xnA", [128, D], BF16, 4)
                uT_pool = Pool(es, nc, "uTA", [128, KD, 512], BF16, 2, 4)
                zs_pool = Pool(es, nc, "zsA", [128, D], BF16, 2, 2)
                rs_pool = Pool(es, nc, "rsA", [128, D], BF16, 2, 2)
                v_pool = Pool(es, nc, "vA", [128, D], BF16, 2, 2)
                k_pool = Pool(es, nc, "kA", [128, 512], BF16, 2)
                dtmp_pool = Pool(es, nc, "dtmpA", [128, 32], F32, 2)
                xbc_pool = Pool(es, nc, "xbcA", [128, 12, 512], BF16, 1, 12)
                qk_pool = Pool(es, nc, "qkA", [128, 4, 8, 128], BF16, 1, 8)
                ga_pool = Pool(es, nc, "gaA", [17, 2, 512], F32, 1, 2)
                for _ in range(1):
                    ga, gab = ga_pool.next()
                    S.op('pool', MS(ga, 1.0), writes=[gab])
                gat3 = GAT.rearrange("d r t -> r d t")
                evi = 0
                def preA(tok0, ntok):
                    xns = []
                    for ci in range(ntok // 128):
                        c = tok0 // 128 + ci
                        hx, hxb = hx_pool.next()
                        S.dma('sp', hx, h_src(c), reads=[(DB('H2'), c)] if l > 0 else [], writes=[hxb])
                        xns.append(norm_pre((ssp, xnp), hx, hxb))
                    return xns

                def peA(xns, tok0):
                    uT_, uTb_ = uT_pool.next()
                    s_ = 1 if tok0 < CTX else 0
                    for ci, (xn, xnb) in enumerate(xns):
                        norm_pe(xn, xnb, ci, uT_, uTb_, A1c, A1_b, 0, s_)
                    return uT_, uTb_

                uTq = {0: peA(preA(*tiles[0]), tiles[0][0])}
                for ti, (tok0, ntok) in enumerate(tiles):
                    nchk = ntok // 128
                    s = 1 if tok0 < CTX else 0
                    uT, uTb = uTq.pop(ti)
                    xns_next = preA(*tiles[ti + 1]) if ti + 1 < len(tiles) else None
                    for ci in range(nchk):
                        c = tok0 // 128 + ci
                        rows = slice(c * 128, (c + 1) * 128)
                        lhs = [uT[:, kc, ci * 128:(ci + 1) * 128] for kc in range(KD)]
                        zs, zsb = zs_pool.next()
                        rs, rsb = rs_pool.next()
                        vv, vvb = v_pool.next()
                        kk, kkb = k_pool.next()
                        for (w0, n, kind, half) in [(0, 512, 'z', 0), (512, 512, 'z', 1), (1024, 512, 'k', 0), (1536, 512, 'v', 0),
                                                    (2048, 512, 'v', 1), (2560, 512, 'r', 0), (3072, 512, 'r', 1), (3584, 32, 'dt', 0)]:
                            pa, pb = palloc()
                            S.op('pe', MM([(pa[:, 0:n], lhs[kc], wtm[:, kc, w0:w0 + n], kc == 0, kc == KD - 1) for kc in range(KD)]),
                                 reads=[(uTb, ci), wtm_b], writes=pb)
                            hs = slice(half * 512, half * 512 + 512)
                            if kind == 'z':
                                S.op('act', ACT(zs[:, hs], pa, AF.Silu), reads=pb, writes=[(zsb, half)])
                            elif kind == 'r':
                                S.op('act', ACT(rs[:, hs], pa, AF.Silu), reads=pb, writes=[(rsb, half)])
                            elif kind == 'v':
                                S.op('dve', CP(vv[:, hs], pa), reads=pb, writes=[(vvb, half)])
                            elif kind == 'k':
                                S.op('dve', CP(kk, pa), reads=pb, writes=[kkb])
                            else:
                                dtm, dtmb = dtmp_pool.next()
                                S.op('dve', TT(dtm, pa[:, 0:32], dtb_bc[:], ALU.add), reads=pb + [dtb_b], writes=[dtmb])
                                S.op('act', ACT(dtm, dtm, AF.Exp), reads=[dtmb], writes=[dtmb])
                                S.op('act', ACT(dt_all[:, c, :], dtm, AF.Ln, bias=1.0), reads=[dtmb], writes=[(dt_b, c)])
                                S.op('dve', TT(a_all[:, c, :], dt_all[:, c, :], A_bc[:], ALU.mult), reads=[(dt_b, c), A_b], writes=[(a_b, c)])
                                S.op('dve', CP(ahl_all[:, c, 0, :], a_all[:, c, :]), reads=[(a_b, c)], writes=[(ahl_b, c)])
                                S.op('dve', TT(ahl_all[:, c, 1, :], a_all[:, c, :], ahl_all[:, c, 0, :], ALU.subtract), reads=[(a_b, c), (ahl_b, c)], writes=[(ahl_b, c)])
                        S.dma('sp', ZS[rows, :], zs, reads=[zsb], writes=[(DB('ZS'), c)])
                        S.dma('sp', RS[rows, :], rs, reads=[rsb], writes=[(DB('RS'), c)])
                        S.dma('sp', V[rows, :], vv, reads=[vvb], writes=[(DB('V'), c)])
                        S.dma('sp', KTM[rows, :], kk, reads=[kkb], writes=[(DB('KTM'), c)])
                    if xns_next is not None:
                        uTq[ti + 1] = peA(xns_next, tiles[ti + 1][0])
                    xbc, xbcb = xbc_pool.next()
                    qk, qkb = qk_pool.next()
                    ga, gab = ga_pool.next()
                    rhs = [uT[:, kc, 0:ntok] for kc in range(KD)]
                    for j in range(20):
                        pa, pb = palloc()
                        S.op('pe', MM([(pa[:, 0:ntok], wfm[:, kc, j * 128:(j + 1) * 128], rhs[kc], kc == 0, kc == KD - 1) for kc in range(KD)]),
                             reads=[uTb, wfm_b], writes=pb)
                        eng = 'act' if (evi % 2 == 0) else 'dve'
                        evi += 1
                        if j < 12:
                            dst, dstb = xbc[:, j, 0:ntok], (xbcb, j)
                        else:
                            dst, dstb = qk[:, 0:nchk, j - 12, :], (qkb, j - 12)
                        scale = (128.0 ** -0.5) if (12 <= j < 16) else 1.0
                        src_ = pa[:, 0:ntok] if j < 12 else pa[:, 0:ntok].rearrange("p (c t) -> p c t", t=128)
                        if eng == 'act':
                            S.op('act', ACT(dst, src_, AF.Copy, scale=scale), reads=pb, writes=[dstb])
                        else:
                            S.op('dve', TS(dst, src_, scale, None, ALU.mult), reads=pb, writes=[dstb])
                    for dd in range(2):
                        pa, pb = palloc()
                        S.op('pe', MM([(pa[0:16, 0:ntok], wfm[:, kc, 2560 + dd * 16:2576 + dd * 16], rhs[kc], kc == 0, kc == KD - 1) for kc in range(KD)]),
                             reads=[uTb, wfm_b], writes=pb)
                        S.op('dve', CP(ga[0:16, dd, 0:ntok], pa[0:16, 0:ntok]), reads=pb, writes=[(gab, dd)])
                    c0 = xcol(tok0)
                    ch = list(range(tok0 // 128, tok0 // 128 + nchk))
                    S.dma('sp', xb3[:, :, c0:c0 + ntok], xbc[:, :, 0:ntok], reads=[xbcb], writes=[DB('XBCT', 1)])
                    for ci, c in enumerate(ch):
                        S.dma('sp', QKB[c], qk[:, ci], reads=[qkb], writes=[(DB('QKB'), c)])
                    S.dma('sp', gat3[:, :, tok0:tok0 + ntok], ga[:, :, 0:ntok], reads=[gab], writes=[(DB('GAT'), c) for c in ch])
                S.flush()
            wes.close()
            if stop_after == 'A':
                les.close()
                break

            with ExitStack() as es:
                dg, dg_b = single(es, nc, "dg", [128, 12, 9, 128], BF16)
                cwc, cwc_b = single(es, nc, "cwc", [128, 12, 9], F32)
                cbc, cbc_b = single(es, nc, "cbc", [128, 12], F32)
                S.dma('sp', cwc[:], convw_col[l], writes=[cwc_b])
                S.dma('sp', cbc[:], convb_col[l], writes=[cbc_b])
                for cc in range(12):
                    S.op('dve' if cc % 2 == 0 else 'pool', TT(dg[:, cc], ident[:].unsqueeze(1).to_broadcast([128, 9, 128]),
                                   cwc[:, cc, :].unsqueeze(2).to_broadcast([128, 9, 128]), ALU.mult),
                         reads=[ident_b, cwc_b], writes=[dg_b])
                xw_pool = Pool(es, nc, "xwB", [128, 12, 768], BF16, 2)
                xl_pool = Pool(es, nc, "xlB", [128, 12, 768], BF16, 2)
                xr_pool = Pool(es, nc, "xrB", [128, 12, 768], BF16, 2)
                fm_pool = Pool(es, nc, "fmB", [128, 10, 512], BF16, 2, 10)
                xs_pool = Pool(es, nc, "xsB", [128, D], BF16, 3)
                bt_pool = Pool(es, nc, "btB", [128, 256], BF16, 3)
                bc_pool = Pool(es, nc, "bcB", [128, 4, 4, 128], BF16, 2, 4)
                def prepB(tok0, ntok):
                    nchk = ntok // 128
                    isctx = tok0 < CTX
                    c0 = xcol(tok0) - 128
                    W = ntok + 256
                    xw, xwb = xw_pool.next()
                    S.dma('sp', xw[:, :, 0:W], xb3[:, :, c0:c0 + W], reads=[DB('XBCT', 1)], writes=[xwb])
                    if isctx:
                        taps = [(3 + dc + 1, xw, xwb, 128 + dc) for dc in (-1, 0, 1)]
                    else:
                        xl, xlb = xl_pool.next()
                        xr, xrb = xr_pool.next()
                        S.op('act', ACT(xl[:, :, 0:W], xw[:, :, 0:W], AF.Copy), reads=[xwb], writes=[xlb])
                        S.op('dve', CP(xr[:, :, 0:W], xw[:, :, 0:W]), reads=[xwb], writes=[xrb])
                        S.op('pool', MS(xl[:, :, 63:W:64], 0.0), reads=[xlb], writes=[xlb])
                        S.op('pool', MS(xr[:, :, 0:W:64], 0.0), reads=[xrb], writes=[xrb])
                        taps = []
                        for dr in (-1, 0, 1):
                            for dc in (-1, 0, 1):
                                src, srcb = (xl, xlb) if dc == -1 else ((xr, xrb) if dc == 1 else (xw, xwb))
                                taps.append(((dr + 1) * 3 + dc + 1, src, srcb, 128 + 64 * dr + dc))
                    return taps

                tapsq = {0: prepB(*tiles[0])}
                for ti, (tok0, ntok) in enumerate(tiles):
                    nchk = ntok // 128
                    if ti + 1 < len(tiles):
                        tapsq[ti + 1] = prepB(*tiles[ti + 1])
                    taps = tapsq.pop(ti)
                    tbufs = [t[2] for t in taps]
                    fm, fmb = fm_pool.next()
                    bc, bcb = bc_pool.next()
                    for cc in range(12):
                        pa, pb = palloc()
                        lst = [(pa[:, 0:ntok], dg[:, cc, k, :], src[:, cc, off:off + ntok], ti == 0, ti == len(taps) - 1)
                               for ti, (k, src, srcb, off) in enumerate(taps)]
                        S.op('pe', MM(lst), reads=tbufs + [dg_b], writes=pb)
                        if cc < 10:
                            S.op('act', ACT(fm[:, cc, 0:ntok], pa[:, 0:ntok], AF.Silu, bias=cbc[:, cc:cc + 1]), reads=pb + [cbc_b], writes=[(fmb, cc)])
                        if cc >= 8:
                            c4 = cc - 8
                            if cc < 10:
                                S.op('dve', CP(bc[:, 0:nchk, c4, :], fm[:, cc, 0:ntok].rearrange("p (c t) -> p c t", t=128)),
                                     reads=[(fmb, cc)], writes=[(bcb, c4)])
                            else:
                                S.op('act', ACT(bc[:, 0:nchk, c4, :], pa[:, 0:ntok].rearrange("p (c t) -> p c t", t=128), AF.Silu, bias=cbc[:, cc:cc + 1]),
                                     reads=pb + [cbc_b], writes=[(bcb, c4)])
                    ch = list(range(tok0 // 128, tok0 // 128 + nchk))
                    for ci, c in enumerate(ch):
                        S.dma('sp', BCB[c], bc[:, ci], reads=[bcb], writes=[(DB('BCB'), c)])
                    for ci, c in enumerate(ch):
                        rows = slice(c * 128, (c + 1) * 128)
                        pa, pb = palloc()
                        pT = pa.bitcast(BF16)
                        S.op('pe', TR([(pT[:, cc * 128:(cc + 1) * 128], fm[:, cc, ci * 128:(ci + 1) * 128], ident[:]) for cc in range(8)]),
                             reads=[(fmb, cc) for cc in range(8)] + [ident_b], writes=pb)
                        xs, xsb = xs_pool.next()
                        S.op('dve', CP(xs, pT), reads=pb, writes=[xsb])
                        S.dma('sp', XS[rows, :], xs, reads=[xsb], writes=[(DB('XS'), c)])
                        pa, pb = palloc()
                        pT = pa.bitcast(BF16)
                        S.op('pe', TR([(pT[:, c2 * 128:(c2 + 1) * 128], fm[:, 8 + c2, ci * 128:(ci + 1) * 128], ident[:]) for c2 in range(2)]),
                             reads=[(fmb, 8), (fmb, 9), ident_b], writes=pb)
                        bt, btb = bt_pool.next()
                        S.op('dve', CP(bt, pT[:, 0:256]), reads=pb, writes=[btb])
                        S.dma('sp', BTM[rows, :], bt, reads=[btb], writes=[(DB('BTM'), c)])
                S.flush()
            if stop_after == 'B':
                les.close()
                break

            with ExitStack() as es:
                w2a, w2a_b = single(es, nc, "w2a", [17, 2, 512], F32)
                S.dma('sp', w2a[:], w2aug_in[l].rearrange("d r n -> r d n"), writes=[w2a_b])
                D_bc, D_b = single(es, nc, "D_bc", [128, 32], F32)
                S.dma('sp', D_bc[:], dskip_in[l].partition_broadcast(128), writes=[D_b])
                Dd, Dd_b = single(es, nc, "Dd", [128, 2, 32, 128], BF16)
                Dhl, Dhl_b = single(es, nc, "Dhl", [128, 2, 32], F32)
                Dh16, Dh16_b = single(es, nc, "Dh16", [128, 32], BF16)
                S.op('dve', CP(Dh16[:], D_bc[:]), reads=[D_b], writes=[Dh16_b])
                S.op('dve', CP(Dhl[:, 0, :], Dh16[:]), reads=[Dh16_b], writes=[Dhl_b])
                S.op('dve', TT(Dhl[:, 1, :], D_bc[:], Dhl[:, 0, :], ALU.subtract), reads=[D_b, Dhl_b], writes=[Dhl_b])
                for hl in range(2):
                    S.op('dve', TT(Dd[:, hl], ident[:].unsqueeze(1).to_broadcast([128, 32, 128]),
                                   Dhl[:, hl, :].unsqueeze(2).to_broadcast([128, 32, 128]), ALU.mult), reads=[ident_b, Dhl_b], writes=[Dd_b])
                pyc_pool = Pool(es, nc, "pycC", [128, D], F32, 2)
                hst, hst_b = single(es, nc, "hst", [128, D], F32)
                hbf, hbf_b = single(es, nc, "hbf", [128, D], BF16)
                Sst, Sst_b = single(es, nc, "Sst", [128, D], F32, 4)
                Sbf, Sbf_b = single(es, nc, "Sbf", [128, D], BF16)
                xs_pool = Pool(es, nc, "xsC", [128, D], BF16, 4)
                btm_pool = Pool(es, nc, "btmC", [128, 256], BF16, 4)
                bct_pool = Pool(es, nc, "bctC", [128, 4, 128], BF16, 4)
                qk_pool = Pool(es, nc, "qkC", [128, 8, 128], BF16, 4)
                ktm_pool = Pool(es, nc, "ktmC", [128, 512], BF16, 3)
                v_pool = Pool(es, nc, "vC", [128, D], BF16, 4)
                ga_pool = Pool(es, nc, "gaC", [17, 128], F32, 3)
                yf_pool = Pool(es, nc, "yfC", [128, D], F32, 3)
                of_pool = Pool(es, nc, "ofC", [128, D], F32, 3)
                sm_pool = Pool(es, nc, "smC", [128, 48], F32, 3)
                rb_pool = Pool(es, nc, "rbC", [128, 2048], F32, 2)
                E_pool = Pool(es, nc, "EC", [128, 2048], BF16, 1)
                cbm_pool = Pool(es, nc, "cbmC", [128, 256], BF16, 2)
                MT_pool = Pool(es, nc, "MTC", [128, 2048], BF16, 2)
                xdt_pool = Pool(es, nc, "xdtC", [128, D], BF16, 2)
                xdd_pool = Pool(es, nc, "xddC", [128, D], BF16, 2)
                yt_pool = Pool(es, nc, "ytC", [128, D], F32, 1)
                ys_pool = Pool(es, nc, "ysC", [128, D], F32, 2)
                sp_pool = Pool(es, nc, "spC", [128, 512], F32, 1)
                sph_pool = Pool(es, nc, "sphC", [128, 2, 512], BF16, 2)
                eq_pool = Pool(es, nc, "eqC", [128, 512], F32, 2)
                ek_pool = Pool(es, nc, "ekC", [128, 512], F32, 1)
                kes_pool = Pool(es, nc, "kesC", [128, 512], F32, 1)
                qd_pool = Pool(es, nc, "qdC", [128, 512], BF16, 2)
                ki_pool = Pool(es, nc, "kiC", [128, 512], BF16, 2)
                ke_pool = Pool(es, nc, "keC", [128, 512], BF16, 2)
                att_pool = Pool(es, nc, "attC", [128, 512], BF16, 2)
                os_pool = Pool(es, nc, "osC", [128, D], F32, 2)

                def b3(ap, a, b_):
                    return ap.rearrange("p (a b) -> p a b", a=a, b=b_)

                def ld(c, dirn, need_out, pass2):
                    X = {'c': c, 'dirn': dirn, 'need': need_out}
                    rows = slice(c * 128, (c + 1) * 128)
                    cols = slice(c * 128, (c + 1) * 128)
                    xs, xsb = xs_pool.next()
                    S.dma('sp', xs, XS[rows, :], reads=[(DB('XS'), c)], writes=[xsb])
                    btm, btmb = btm_pool.next()
                    S.dma('sp', btm, BTM[rows, :], reads=[(DB('BTM'), c)], writes=[btmb])
                    bct, bctb = bct_pool.next()
                    S.dma('sp', bct, BCB[c], reads=[(DB('BCB'), c)], writes=[bctb])
                    qk, qkb = qk_pool.next()
                    S.dma('sp', qk, QKB[c], reads=[(DB('QKB'), c)], writes=[qkb])
                    ktm, ktmb = ktm_pool.next()
                    S.dma('sp', ktm, KTM[rows, :], reads=[(DB('KTM'), c)], writes=[ktmb])
                    vv, vvb = v_pool.next()
                    S.dma('sp', vv, V[rows, :], reads=[(DB('V'), c)], writes=[vvb])
                    ga, gab = ga_pool.next()
                    S.dma('sp', ga, GAT[dirn, :, cols], reads=[(DB('GAT'), c)], writes=[gab])
                    X.update(xs=xs, xsb=xsb, btm=btm, btmb=btmb, bct=bct, bctb=bctb, qk=qk, qkb=qkb, ktm=ktm, ktmb=ktmb, vv=vv, vvb=vvb, ga=ga, gab=gab)
                    return X

                def p2(X, pass2):
                    c, dirn = X['c'], X['dirn']
                    rows = slice(c * 128, (c + 1) * 128)
                    Uc16 = U16[:, dirn, :]
                    Ws16 = U16[:, 2 + dirn, :]
                    ahi = ahl_all[:, c, 0, dirn * 16:(dirn + 1) * 16]
                    alo = ahl_all[:, c, 1, dirn * 16:(dirn + 1) * 16]
                    ga, gab = X['ga'], X['gab']
                    if pass2 and X['need']:
                        of, ofb = of_pool.next()
                        S.dma('sp', of, OF[rows, :], reads=[(DB('OF'), c)], writes=[ofb])
                        yf, yfb = yf_pool.next()
                        S.dma('sp', yf, YF[rows, :], reads=[(DB('YF'), c)], writes=[yfb])
                        X.update(of=of, ofb=ofb, yf=yf, yfb=yfb)
                    pa, pb = palloc()
                    lst = []
                    for k3, m in enumerate((Uc16, Ws16, ones16[:])):
                        lst.append((pa[:, k3 * 16:(k3 + 1) * 16], m, ahi, True, False))
                        lst.append((pa[:, k3 * 16:(k3 + 1) * 16], m, alo, False, True))
                    S.op('pe', MM(lst), reads=[U16_b, ones16_b, (ahl_b, c)], writes=pb)
                    sm, smb = sm_pool.next()
                    S.op('act', ACT(sm, pa[:, 0:48], AF.Exp), reads=pb, writes=[smb])
                    rbh, rbhb = rb_pool.next()
                    rbl, rblb = None, None
                    S.op('dve', TT(b3(rbh, 16, 128), Umat[:, dirn, :].unsqueeze(1).to_broadcast([128, 16, 128]),
                                   a_all[:, c, dirn * 16:(dirn + 1) * 16].unsqueeze(2).to_broadcast([128, 16, 128]), ALU.mult),
                         reads=[U_b, (a_b, c)], writes=[rbhb])
                    pa, pb = palloc()
                    S.op('pe', MM([(pa, ga[0:17, :], w2a[0:17, dirn, :], True, True)]), reads=[gab, w2a_b], writes=pb)
                    spt, sptb = sp_pool.next()
                    S.op('act', ACT(spt, pa, AF.Exp, scale=-1.0), reads=pb, writes=[sptb])
                    S.op('act', ACT(spt, spt, AF.Ln, bias=1.0), reads=[sptb], writes=[sptb])
                    sph, sphb = sph_pool.next()
                    S.op('act', ACT(sph[:, 0, :], spt, AF.Copy), reads=[sptb], writes=[sphb])
                    S.op('pool', TT(sph[:, 1, :], spt, sph[:, 0, :], ALU.subtract), reads=[sptb, sphb], writes=[sphb])
                    X.update(sm=sm, smb=smb, rbh=rbh, rbhb=rbhb, rbl=rbl, rblb=rblb, sph=sph, sphb=sphb)

                def p3(X):
                    c, dirn, need_out = X['c'], X['dirn'], X['need']
                    Uc = Umat[:, dirn, :]
                    Uc16 = U16[:, dirn, :]
                    Ws16 = U16[:, 2 + dirn, :]
                    dtc = dt_all[:, c, dirn * 16:(dirn + 1) * 16]
                    xs, xsb, bct, bctb, qk, qkb, ktm, ktmb = X['xs'], X['xsb'], X['bct'], X['bctb'], X['qk'], X['qkb'], X['ktm'], X['ktmb']
                    sm, smb, rbh, rbhb, rbl, rblb, sph, sphb = X['sm'], X['smb'], X['rbh'], X['rbhb'], X['rbl'], X['rblb'], X['sph'], X['sphb']
                    qt, ktp = qk[:, 0:4, :], qk[:, 4:8, :]
                    pa, pb = palloc()
                    lst = []
                    for hh in range(4):
                        lst.append((pa[:, hh * 128:(hh + 1) * 128], sph[:, 0, hh * 128:(hh + 1) * 128], Uc16, True, False))
                        lst.append((pa[:, hh * 128:(hh + 1) * 128], sph[:, 1, hh * 128:(hh + 1) * 128], Uc16, False, True))
                    S.op('pe', MM(lst), reads=[sphb, U16_b], writes=pb)
                    eq, eqb = eq_pool.next()
                    ek, ekb = ek_pool.next()
                    S.op('act', ACT(eq, pa, AF.Exp, scale=-1.0 / 16), reads=pb, writes=[eqb])
                    S.op('act', ACT(ek, pa, AF.Exp, scale=1.0 / 16), reads=pb, writes=[ekb])
                    pa, pb = palloc()
                    S.op('pe', MM([(pa, Ws16, sph[:, 0, :], True, False), (pa, Ws16, sph[:, 1, :], False, True)]), reads=[sphb, U16_b], writes=pb)
                    kes, kesb = kes_pool.next()
                    S.op('act', ACT(kes, pa, AF.Exp, scale=-1.0 / 16), reads=pb, writes=[kesb])
                    pa, pb = palloc()
                    S.op('pe', MM([(pa[:, g * 128:(g + 1) * 128], bct[:, g, :], bct[:, 2 + g, :], True, True) for g in range(2)]),
                         reads=[bctb], writes=pb)
                    cbm, cbmb = cbm_pool.next()
                    S.op('dve', TT(b3(cbm, 2, 128), b3(pa[:, 0:256], 2, 128), Uc.unsqueeze(1).to_broadcast([128, 2, 128]), ALU.mult),
                         reads=pb + [U_b], writes=[cbmb])
                    pD, pDb = palloc(4)
                    lst = []
                    for q in range(4):
                        lst.append((pD[:, q * 512:(q + 1) * 512], Umat[:, 2 + dirn, :], rbh[:, q * 512:(q + 1) * 512], True, True))
                    S.op('pe', MM(lst), reads=[U_b, rbhb], writes=pDb)
                    xdt, xdtb = xdt_pool.next()
                    S.op('pool', TT(b3(xdt, 16, 64), b3(xs, 16, 64), dtc.unsqueeze(2).to_broadcast([128, 16, 64]), ALU.mult),
                         reads=[xsb, (dt_b, c)], writes=[xdtb])
                    xdd, xddb = xdd_pool.next()
                    S.op('pool', TT(b3(xdd, 16, 64), b3(xdt, 16, 64), sm[:, 16:32].unsqueeze(2).to_broadcast([128, 16, 64]), ALU.mult),
                         reads=[xdtb, smb], writes=[xddb])
                    qd, qdb = qd_pool.next()
                    ki, kib = ki_pool.next()
                    ke, keb = ke_pool.next()
                    S.op('dve', TT(qd, qt.rearrange("p a b -> p (a b)"), eq, ALU.mult), reads=[qkb, eqb], writes=[qdb])
                    S.op('dve', TT(ki, ktp.rearrange("p a b -> p (a b)"), ek, ALU.mult), reads=[qkb, ekb], writes=[kib])
                    S.op('pool', TT(ke, ktm, kes, ALU.mult), reads=[ktmb, kesb], writes=[keb])
                    E, Eb = E_pool.next()
                    S.op('act', ACT(E, pD, AF.Exp), reads=pDb, writes=[Eb])
                    att = attb = None
                    if need_out:
                        pa, pb = palloc()
                        S.op('pe', MM([(pa[:, hh * 128:(hh + 1) * 128], ki[:, hh * 128:(hh + 1) * 128], qd[:, hh * 128:(hh + 1) * 128], True, True)
                                       for hh in range(4)]), reads=[kib, qdb], writes=pb)
                        att, attb = att_pool.next()
                        S.op('dve', TT(b3(att, 4, 128), b3(pa, 4, 128), Uc.unsqueeze(1).to_broadcast([128, 4, 128]), ALU.mult),
                             reads=pb + [U_b], writes=[attb])
                    MT, MTb = MT_pool.next()
                    for g in range(2):
                        S.op('dve' if g == 0 else 'pool',
                             TT(b3(MT[:, g * 1024:(g + 1) * 1024], 8, 128), b3(E[:, g * 1024:(g + 1) * 1024], 8, 128),
                                cbm[:, g * 128:(g + 1) * 128].unsqueeze(1).to_broadcast([128, 8, 128]), ALU.mult),
                             reads=[Eb, cbmb], writes=[MTb])
                    X.update(MT=MT, MTb=MTb, xdt=xdt, xdtb=xdtb, xdd=xdd, xddb=xddb, eq=eq, eqb=eqb, qd=qd, qdb=qdb, ke=ke, keb=keb, att=att, attb=attb)

                def finish_a(X, pass2):
                    c, dirn, need_out = X['c'], X['dirn'], X['need']
                    rows = slice(c * 128, (c + 1) * 128)
                    xs, xsb, btm, btmb, bct, bctb, vv, vvb = X['xs'], X['xsb'], X['btm'], X['btmb'], X['bct'], X['bctb'], X['vv'], X['vvb']
                    sm, smb, MT, MTb, xdt, xdtb, xdd, xddb = X['sm'], X['smb'], X['MT'], X['MTb'], X['xdt'], X['xdtb'], X['xdd'], X['xddb']
                    eq, eqb, qd, qdb, ke, keb, att, attb = X['eq'], X['eqb'], X['qd'], X['qdb'], X['ke'], X['keb'], X['att'], X['attb']
                    if need_out:
                        py, pyb = palloc(2)
                        lst = []
                        for h in range(16):
                            o = py[:, h * 64:(h + 1) * 64]
                            lst.append((o, MT[:, h * 128:(h + 1) * 128], xdt[:, h * 64:(h + 1) * 64], True, False))
                            lst.append((o, Dd[:, 0, dirn * 16 + h, :], xs[:, h * 64:(h + 1) * 64], False, False))
                            lst.append((o, Dd[:, 1, dirn * 16 + h, :], xs[:, h * 64:(h + 1) * 64], False, True))
                        S.op('pe', MM(lst), reads=[MTb, xdtb, xsb, Dd_b], writes=pyb)
                        pyo, pyob = palloc(2)
                        S.op('pe', MM([(pyo[:, g * 512:(g + 1) * 512], bct[:, 2 + g, :], hbf[:, g * 512:(g + 1) * 512], True, True)
                                       for g in range(2)]), reads=[bctb, hbf_b], writes=pyob)
                        ys, ysb = ys_pool.next()
                        S.op('dve', TT(b3(ys, 16, 64), b3(pyo, 16, 64), sm[:, 0:16].unsqueeze(2).to_broadcast([128, 16, 64]), ALU.mult),
                             reads=pyob + [smb], writes=[ysb])
                        pyc, pycb = pyc_pool.next()
                        S.op('act', ACT(pyc, py, AF.Copy), reads=pyb, writes=[pycb])
                        S.op('pool', TT(ys, ys, pyc, ALU.add), reads=[ysb, pycb], writes=[ysb])
                        X.update(ys=ys, ysb=ysb)
                    pst_, pstb = palloc(2)
                    S.op('pe', MM([(pst_[:, g * 512:(g + 1) * 512], btm[:, g * 128:(g + 1) * 128], xdd[:, g * 512:(g + 1) * 512], True, True)
                                   for g in range(2)]), reads=[btmb, xddb], writes=pstb)
                    S.op('dve', TT(b3(hst[:], 16, 64), b3(hst[:], 16, 64), sm[:, 32:48].unsqueeze(2).to_broadcast([128, 16, 64]), ALU.mult),
                         reads=[hst_b, smb], writes=[hst_b])
                    S.op('dve', TT(hst[:], hst[:], pst_, ALU.add), reads=[hst_b] + pstb, writes=[hst_b])
                    S.op('act', ACT(hbf[:], hst[:], AF.Copy), reads=[hst_b], writes=[hbf_b])
                    if need_out:
                        po, pob = palloc(2)
                        lst = []
                        for hh in range(4):
                            o = po[:, hh * 256:(hh + 1) * 256]
                            lst.append((o, att[:, hh * 128:(hh + 1) * 128], vv[:, hh * 256:(hh + 1) * 256], True, False))
                            lst.append((o, qd[:, hh * 128:(hh + 1) * 128], Sbf[:, hh * 256:(hh + 1) * 256], False, True))
                        S.op('pe', MM(lst), reads=[attb, vvb, qdb, Sbf_b], writes=pob)
                        osum, osb = os_pool.next()
                        if not pass2:
                            S.op('act', ACT(osum, po, AF.Copy), reads=pob, writes=[osb])
                        else:
                            S.op('dve', TT(osum, po, X['of'], ALU.add), reads=pob + [X['ofb']], writes=[osb])
                            S.op('pool', TT(X['ys'], X['ys'], X['yf'], ALU.add), reads=[X['ysb'], X['yfb']], writes=[X['ysb']])
                        X.update(po=po, pob=pob, osum=osum, osb=osb)
                    pu, pub = palloc(2)
                    S.op('pe', MM([(pu[:, hh * 256:(hh + 1) * 256], ke[:, hh * 128:(hh + 1) * 128], vv[:, hh * 256:(hh + 1) * 256], True, True)
                                   for hh in range(4)]), reads=[keb, vvb], writes=pub)
                    lastcol = 127 if dirn == 0 else 0
                    for hh in range(4):
                        S.op('dve', STT(Sst[:, hh * 256:(hh + 1) * 256], Sst[:, hh * 256:(hh + 1) * 256],
                                        eq[:, hh * 128 + lastcol:hh * 128 + lastcol + 1], pu[:, hh * 256:(hh + 1) * 256], ALU.mult, ALU.add),
                             reads=[(Sst_b, hh), eqb] + pub, writes=[(Sst_b, hh)])
                    S.op('act', ACT(Sbf[:], Sst[:], AF.Copy), reads=[Sst_b], writes=[Sbf_b])
                    if need_out:
                        S.dma('sp', YF[rows, :], X['ys'], reads=[X['ysb']], writes=[(DB('YF'), c)])
                        S.dma('sp', OF[rows, :], X['osum'], reads=[X['osb']], writes=[(DB('OF'), c)])

                for pass2 in (False, True):
                    S.op('pool', MS(hst[:], 0.0), reads=[hst_b], writes=[hst_b])
                    S.op('pool', MS(hbf[:], 0.0), reads=[hbf_b], writes=[hbf_b])
                    S.op('pool', MS(Sst[:], 0.0), reads=[Sst_b], writes=[Sst_b])
                    S.op('pool', MS(Sbf[:], 0.0), reads=[Sbf_b], writes=[Sbf_b])
                    order = list(range(NCH)) if not pass2 else [1, 0] + list(range(NCH - 1, 1, -1))
                    dirn = 1 if pass2 else 0
                    needs = [not (last and c < 2) for c in order]
                    n_ = len(order)
                    Xq = {}
                    for it in range(n_ + 3):
                        if it < n_:
                            Xq[it] = ld(order[it], dirn, needs[it], pass2)
                        if 0 <= it - 1 < n_:
                            p2(Xq[it - 1], pass2)
                        if 0 <= it - 2 < n_:
                            p3(Xq[it - 2])
                        if 0 <= it - 3 < n_:
                            finish_a(Xq[it - 3], pass2)
                            del Xq[it - 3]
                S.flush()

            with ExitStack() as es:
                wo3 = w_out[l].rearrange("(kc p) n -> p kc n", p=128)
                wst_pool = Pool(es, nc, "wstC2", [128, 1024], F32, 3)
                gcol, gcol_b = single(es, nc, "gcol", [128, 16], F32)
                S.dma('sp', gcol[:, 0:8], ssdg_col_in[l], writes=[gcol_b])
                for hh in range(4):
                    S.dma('sp', gcol[:, 8 + 2 * hh:10 + 2 * hh], glag_col_in[l], writes=[gcol_b])
                g1bc, g1bc_b = single(es, nc, "g1bc", [128, 2, D], F32)
                S.dma('sp', g1bc[:], GBC[0].partition_broadcast(128), reads=[DB('GBC', 1)], writes=[g1bc_b])
                nstream = 1 if last else 2
                wouts = []
                for sidx in range(nstream):
                    wo_t, wo_b = single(es, nc, "wout%d" % sidx, [128, 16, D], BF16)
                    wouts.append((wo_t, wo_b))
                for kc in range(16):
                    st_, stb_ = wst_pool.next()
                    S.dma('sp', st_, wo3[:, kc, :], writes=[stb_])
                    for sidx in range(nstream):
                        wo_t, wo_b = wouts[sidx]
                        S.op('dve',
                             STT(wo_t[:, kc, :], st_, gcol[:, kc:kc + 1], g1bc[:, sidx, :], ALU.mult, ALU.mult),
                             reads=[stb_, gcol_b, g1bc_b], writes=[wo_b])
                NB = 4
                ys_pool = Pool(es, nc, "ysE", [128, D], F32, NB)
                os_pool = Pool(es, nc, "osE", [128, D], F32, NB)
                zs_pool = Pool(es, nc, "zsE", [128, D], BF16, NB)
                rs_pool = Pool(es, nc, "rsE", [128, D], BF16, NB)
                hx_pool = Pool(es, nc, "hxE", [128, D], F32, NB + 2)
                st_pool = Pool(es, nc, "stE", [128, 8], F32, NB + 1)
                hd_pool = Pool(es, nc, "hdE", [128, 2 * D], BF16, 2, 2)
                hT_pool = Pool(es, nc, "hTE", [128, 16, 128], BF16, 2, 2)
                junkp = Pool(es, nc, "junkE", [128, D], BF16, 1)

                def b3(ap, a, b_):
                    return ap.rearrange("p (a b) -> p a b", a=a, b=b_)

                def e0(c):
                    X = {'c': c}
                    rows = slice(c * 128, (c + 1) * 128)
                    ys, ysb = ys_pool.next()
                    S.dma('sp', ys, YF[rows, :], reads=[(DB('YF'), c)], writes=[ysb])
                    osum, osb = os_pool.next()
                    S.dma('sp', osum, OF[rows, :], reads=[(DB('OF'), c)], writes=[osb])
                    zs, zsb = zs_pool.next()
                    S.dma('sp', zs, ZS[rows, :], reads=[(DB('ZS'), c)], writes=[zsb])
                    rs, rsb = rs_pool.next()
                    S.dma('sp', rs, RS[rows, :], reads=[(DB('RS'), c)], writes=[rsb])
                    hx, hxb = hx_pool.next()
                    S.dma('sp', hx, h_src(c), reads=[(DB('H2'), c)] if l > 0 else [], writes=[hxb])
                    X.update(ys=ys, ysb=ysb, osum=osum, osb=osb, zs=zs, zsb=zsb, rs=rs, rsb=rsb, hx=hx, hxb=hxb)
                    return X

                def e1(X):
                    ys, ysb, osum, osb, zs, zsb = X['ys'], X['ysb'], X['osum'], X['osb'], X['zs'], X['zsb']
                    st, stb = st_pool.next()
                    S.op('pool', MS(st, 0.0), writes=[stb])
                    junk, junkb = junkp.next()
                    S.op('pool', TT(ys, ys, zs, ALU.mult), reads=[ysb, zsb], writes=[ysb])
                    for g in range(2):
                        S.op('act', ACT(junk[:, g * 512:(g + 1) * 512], ys[:, g * 512:(g + 1) * 512], AF.Square, accum_out=st[:, g:g + 1]),
                             reads=[ysb, stb], writes=[stb])
                    for hh in range(4):
                        S.op('act', ACT(junk[:, hh * 256:(hh + 1) * 256], osum[:, hh * 256:(hh + 1) * 256], AF.Square, accum_out=st[:, 2 + hh:3 + hh]),
                             reads=[osb, stb], writes=[stb])
                    X.update(st=st, stb=stb)

                def e2(X):
                    ys, ysb, osum, osb, rs, rsb, st, stb = X['ys'], X['ysb'], X['osum'], X['osb'], X['rs'], X['rsb'], X['st'], X['stb']
                    S.op('dve', TS(st[:, 0:2], st[:, 0:2], 1.0 / 512, EPS, ALU.mult, ALU.add), reads=[stb], writes=[stb])
                    S.op('dve', TS(st[:, 2:6], st[:, 2:6], 1.0 / 256, EPS, ALU.mult, ALU.add), reads=[stb], writes=[stb])
                    S.op('act', ACT(st[:, 0:6], st[:, 0:6], AF.Sqrt), reads=[stb], writes=[stb])
                    S.op('dve', RECIP(st[:, 0:6], st[:, 0:6]), reads=[stb], writes=[stb])
                    hd, hdb = hd_pool.next()
                    for g in range(2):
                        S.op('act', ACT(hd[:, g * 512:(g + 1) * 512], ys[:, g * 512:(g + 1) * 512], AF.Copy, scale=st[:, g:g + 1]),
                             reads=[ysb, stb], writes=[(hdb, 0)])
                    for hh in range(4):
                        S.op('dve',
                             STT(hd[:, D + hh * 256:D + (hh + 1) * 256], osum[:, hh * 256:(hh + 1) * 256], st[:, 2 + hh:3 + hh],
                                 rs[:, hh * 256:(hh + 1) * 256], ALU.mult, ALU.mult),
                             reads=[osb, stb, rsb], writes=[(hdb, 1)])
                    X.update(hd=hd, hdb=hdb)

                def e3(X):
                    c = X['c']
                    rows = slice(c * 128, (c + 1) * 128)
                    sidx = 1 if c < 2 else 0
                    wo_t, wo_b = wouts[sidx]
                    hd, hdb, hx, hxb = X['hd'], X['hdb'], X['hx'], X['hxb']
                    hT, hTb = hT_pool.next()
                    for half in range(2):
                        pa, pb = palloc()
                        pT = pa.bitcast(BF16)
                        S.op('pe', TR([(pT[:, k8 * 128:(k8 + 1) * 128], hd[:, (half * 8 + k8) * 128:(half * 8 + k8 + 1) * 128], ident[:])
                                       for k8 in range(8)]), reads=[(hdb, half), ident_b], writes=pb)
                        dst = hT[:, half * 8:(half + 1) * 8, :].rearrange("p a b -> p (a b)")
                        if half == 0:
                            S.op('act', ACT(dst, pT, AF.Copy), reads=pb, writes=[(hTb, half)])
                        else:
                            S.op('dve', CP(dst, pT), reads=pb, writes=[(hTb, half)])
                    for nh in range(2):
                        pa, pb = palloc()
                        S.op('pe', MM([(pa, hT[:, kc, :], wo_t[:, kc, nh * 512:(nh + 1) * 512], kc == 0, kc == 15) for kc in range(16)]),
                             reads=[hTb, wo_b], writes=pb)
                        S.op('dve', TT(hx[:, nh * 512:(nh + 1) * 512], pa, hx[:, nh * 512:(nh + 1) * 512], ALU.add), reads=pb + [hxb], writes=[hxb])
                    S.dma('sp', H1[rows, :], hx, reads=[hxb], writes=[(DB('H1'), c)])

                chs = [c for c in range(NCH) if not (last and c < 2)]
                Xs = {}
                for k in range(len(chs) + 3):
                    if k < len(chs):
                        Xs[k] = e0(chs[k])
                    if 0 <= k - 1 < len(chs):
                        e1(Xs[k - 1])
                    if 0 <= k - 2 < len(chs):
                        e2(Xs[k - 2])
                    if 0 <= k - 3 < len(chs):
                        e3(Xs[k - 3])
                        del Xs[k - 3]
                S.flush()
            if stop_after == 'C':
                les.close()
                break

            les.close()
            with ExitStack() as es:
                w1, w1_b = single(es, nc, "w1", [128, KD, 4 * D], BF16, 4)
                w2, w2_b = single(es, nc, "w2", [128, 32, D], BF16)
                w13 = w_ff1[l].rearrange("(kc p) n -> p kc n", p=128)
                w23 = w_ff2[l].rearrange("(kc p) n -> p kc n", p=128)
                wst_pool = Pool(es, nc, "wstD", [128, 512], F32, 4)
                for blk in range(4):
                    for kc in range(KD):
                        load_cast(wst_pool, w1[:, kc, blk * 1024:(blk + 1) * 1024], w13[:, kc, blk * 1024:(blk + 1) * 1024], (w1_b, blk), kc + blk)
                for kc in range(32):
                    load_cast(wst_pool, w2[:, kc, :], w23[:, kc, :], w2_b, kc)
                g2bc, g2bc_b = single(es, nc, "g2bc", [128, 2, D], F32)
                S.dma('sp', g2bc[:], GBC[1].partition_broadcast(128), reads=[DB('GBC', 1)], writes=[g2bc_b])
                if last:
                    fng, fng_b = single(es, nc, "fng", [128, D], F32)
                    S.dma('sp', fng[:], fng_in.partition_broadcast(128), writes=[fng_b])
                TD = 256
                hx_pool = Pool(es, nc, "hxD", [128, D], F32, 4)
                ssp = Pool(es, nc, "ssD", [128, 4], F32, 4)
                xnp = Pool(es, nc, "xnD", [128, D], BF16, 2)
                uT_pool = Pool(es, nc, "uTD", [128, KD, TD], BF16, 2, 2)
                hid_pool = Pool(es, nc, "hidD", [128, 32, TD], BF16, 1, 32)
                rl_pool = Pool(es, nc, "rlD", [128, TD], F32, 3)
                g_pool = Pool(es, nc, "gD", [128, 512], F32, 2)
                st_pool = Pool(es, nc, "stD", [128, 4], F32, 2)
                tilesD = [(t0, TD) for t0 in range(0, T, TD)]
                tilesD = [(t0, n_) for (t0, n_) in tilesD if not (last and t0 < CTX)]

                def preD(tok0, ntok):
                    hxs_, xns_ = [], []
                    for ci in range(ntok // 128):
                        c = tok0 // 128 + ci
                        hx, hxb = hx_pool.next()
                        S.dma('sp', hx, H1[c * 128:(c + 1) * 128, :], reads=[(DB('H1'), c)], writes=[hxb])
                        xns_.append(norm_pre((ssp, xnp), hx, hxb))
                        hxs_.append((hx, hxb))
                    return hxs_, xns_

                def peD(xns_, tok0):
                    uT_, uTb_ = uT_pool.next()
                    s_ = 1 if tok0 < CTX else 0
                    for ci, (xn, xnb) in enumerate(xns_):
                        norm_pe(xn, xnb, ci, uT_, uTb_, A2c, A2_b, 24, s_)
                    return uT_, uTb_

                h0_, x0_ = preD(*tilesD[0])
                dq = {0: (h0_, peD(x0_, tilesD[0][0]))}
                for ti, (tok0, ntok) in enumerate(tilesD):
                    nchk = ntok // 128
                    s = 1 if tok0 < CTX else 0
                    hxs, (uT, uTb) = dq.pop(ti)
                    nxt = preD(*tilesD[ti + 1]) if ti + 1 < len(tilesD) else None
                    hid, hidb = hid_pool.next()
                    for hc in range(32):
                        pa, pb = palloc()
                        S.op('pe', MM([(pa[:, 0:ntok], w1[:, kc, hc * 128:(hc + 1) * 128], uT[:, kc, 0:ntok], kc == 0, kc == KD - 1) for kc in range(KD)]),
                             reads=[uTb, (w1_b, hc // 8)], writes=pb)
                        rl, rlb = rl_pool.next()
                        S.op('act', ACT(rl[:, 0:ntok], pa[:, 0:ntok], AF.Relu), reads=pb, writes=[rlb])
                        S.op('pool' if hc % 2 == 0 else 'dve', TT(hid[:, hc, 0:ntok], rl[:, 0:ntok], rl[:, 0:ntok], ALU.mult), reads=[rlb], writes=[(hidb, hc)])
                    if nxt is not None:
                        dq[ti + 1] = (nxt[0], peD(nxt[1], tilesD[ti + 1][0]))
                    for ci in range(nchk):
                        c = tok0 // 128 + ci
                        hx, hxb = hxs[ci]
                        for nh in range(2):
                            pa, pb = palloc()
                            S.op('pe', MM([(pa, hid[:, hc, ci * 128:(ci + 1) * 128], w2[:, hc, nh * 512:(nh + 1) * 512], hc == 0, hc == 31) for hc in range(32)]),
                                 reads=[hidb, w2_b], writes=pb)
                            gt, gtb = g_pool.next()
                            S.op('dve', TT(gt, pa, g2bc[:, s, nh * 512:(nh + 1) * 512], ALU.mult), reads=pb + [g2bc_b], writes=[gtb])
                            S.op('pool', TT(hx[:, nh * 512:(nh + 1) * 512], hx[:, nh * 512:(nh + 1) * 512], gt, ALU.add), reads=[gtb, hxb], writes=[hxb])
                        if not last:
                            S.dma('sp', H2[c * 128:(c + 1) * 128, :], hx, reads=[hxb], writes=[(DB('H2'), c)])
                        else:
                            st, stb = st_pool.next()
                            xn, xnb = xnp.next()
                            S.op('dve', MS(st[:, 0:1], 0.0), writes=[stb])
                            S.op('act', ACT(xn, hx, AF.Square, accum_out=st[:, 0:1]), reads=[hxb, stb], writes=[xnb, stb])
                            S.op('act', ACT(st[:, 1:2], st[:, 0:1], AF.Sqrt, scale=1.0 / D, bias=eps_t[:, 0:1]), reads=[stb, eps_b], writes=[stb])
                            S.op('dve', RECIP(st[:, 2:3], st[:, 1:2]), reads=[stb], writes=[stb])
                            S.op('dve', STT(hx, hx, st[:, 2:3], fng[:], ALU.mult, ALU.mult), reads=[hxb, stb, fng_b], writes=[hxb])
                            S.dma('sp', out[(c - 2) * 128:(c - 1) * 128, :], hx, reads=[hxb], writes=[DB('OUT')])
                S.flush()
    print("ops recorded:", S.nops)
    return nc


def make_in_maps(inputs, L, ncores):
    f = lambda a: np.ascontiguousarray(np.asarray(a, dtype=np.float32))
    col = lambda v, n: f(np.asarray(v).reshape(n, 128).T)
    shared = {
        "w_ada": f(inputs["w_ada"]),
        "bada_row": f(np.asarray(inputs["b_ada"]).reshape(DEPTH, 1, 6 * D)),
        "bada_col": f(np.stack([col(inputs["b_ada"][l], 48) for l in range(DEPTH)])),
        "n1g_col": f(np.stack([col(inputs["norm1_g"][l], KD) for l in range(DEPTH)])),
        "n2g_col": f(np.stack([col(inputs["norm2_g"][l], KD) for l in range(DEPTH)])),
        "w_in": f(inputs["w_in"]),
        "convw_col": f(np.asarray(inputs["conv_w"]).reshape(DEPTH, 9, 12, 128).transpose(0, 3, 2, 1)),
        "convb_col": f(np.asarray(inputs["conv_b"]).reshape(DEPTH, 12, 128).transpose(0, 2, 1)),
        "convb_row": f(np.asarray(inputs["conv_b"]).reshape(DEPTH, 1, 1536)),
        "dt_bias": f(np.asarray(inputs["dt_bias"]).reshape(DEPTH, 32)),
        "a_log": f(np.asarray(inputs["a_log"]).reshape(DEPTH, 32)),
        "d_skip": f(np.asarray(inputs["d_skip"]).reshape(DEPTH, 32)),
        "ssdg_col": f(np.stack([col(inputs["ssd_norm_g"][l], 8) for l in range(DEPTH)])),
        "glag_col": f(np.stack([col(inputs["gla_norm_g"][l], 2) for l in range(DEPTH)])),
        "w2aug": f(np.concatenate([np.asarray(inputs["gla_w2"]), np.asarray(inputs["gla_b2"])[:, :, None, :]], axis=2)),
        "w_out": f(inputs["w_out"]),
        "w_ff1": f(inputs["w_ff1"]),
        "w_ff2": f(inputs["w_ff2"]),
        "final_norm_g": f(inputs["final_norm_g"]),
    }
    maps = []
    cc = col(inputs["c_ctx"], KD)
    for b in range(ncores):
        m = dict(shared)
        m["x"] = f(inputs["x"][b])
        m["ctx"] = f(inputs["ctx"][b])
        m["ccol"] = f(np.stack([col(inputs["c"][b], KD), cc], axis=-1))
        maps.append(m)
    return maps


_NC_CACHE = {}


def kernel(**inputs):
    x = np.asarray(inputs["x"])
    B, L, _ = x.shape
    if L not in _NC_CACHE:
        _NC_CACHE[L] = build(L)
    nc = _NC_CACHE[L]
    maps = make_in_maps(inputs, L, B)
    res = run_bass_kernel_spmd(nc, maps, core_ids=list(range(B)))
    return np.stack([np.asarray(r["out"], dtype=np.float32) for r in res.results], axis=0)
```
